# Optimizing a Trainium2 kernel written in Bass

```python
import jax, jax.numpy as jnp
from jax import lax
import numpy as np

D_MODEL = 2048
BATCH = 8
SEQ = 4096
DEPTH = 4

CHUNK = 64
N_MIXERS = 2
N_A = (DEPTH + N_MIXERS - 1) // N_MIXERS
N_B = DEPTH // N_MIXERS
NORM_EPS = 1e-6

SGU_BLOCK = 128
SGU_WIDTH = 2 * D_MODEL
SGU_GROUPS = 16
SGU_GROUP_DIM = SGU_WIDTH // SGU_GROUPS

MLA_HEADS = 16
Q_LORA_RANK = 448
KV_LORA_RANK = 512
QK_NOPE_DIM = 128
QK_ROPE_DIM = 64
V_HEAD_DIM = 128
MLA_WIDTH = MLA_HEADS * V_HEAD_DIM
ROPE_THETA = 10000.0
Q_BLOCK = 128

kernel_name = "hybrid_sgu_mla_adaln_sandwich"


def rms_norm(x, g):
    xf = x.astype(jnp.float32)
    y = xf * lax.rsqrt(jnp.mean(xf * xf, axis=-1, keepdims=True) + NORM_EPS)
    return (y * g.astype(jnp.float32)).astype(x.dtype)


def layer_norm(x, g):
    xf = x.astype(jnp.float32)
    mu = jnp.mean(xf, axis=-1, keepdims=True)
    var = jnp.mean(jnp.square(xf - mu), axis=-1, keepdims=True)
    return ((xf - mu) * lax.rsqrt(var + NORM_EPS) * g.astype(jnp.float32)).astype(x.dtype)


def apply_rope(x, cos, sin):
    xf = x.astype(jnp.float32)
    x1, x2 = jnp.split(xf, 2, axis=-1)
    return jnp.concatenate([x1 * cos - x2 * sin, x1 * sin + x2 * cos], axis=-1).astype(x.dtype)


def sgu_mixer(h, w_in, norm_g, w_s, b_s, w_out):
    B, S, _ = h.shape
    u, v, z = jnp.split(h @ w_in, 3, axis=-1)
    u = jax.nn.gelu(u, approximate=False)
    v = layer_norm(jax.nn.gelu(v, approximate=False), norm_g)
    t_chunk = jnp.arange(SGU_BLOCK) // CHUNK
    mask = (t_chunk[None, :] <= t_chunk[:, None]).astype(w_s.dtype)
    w = w_s * mask[None]
    vb = v.reshape(B, S // SGU_BLOCK, SGU_BLOCK, SGU_GROUPS, SGU_GROUP_DIM)
    vm = jnp.einsum('gts,bnsgc->bntgc', w, vb) + b_s.T[:, :, None]
    y = u * vm.reshape(B, S, SGU_WIDTH) * jax.nn.silu(z)
    return y @ w_out


def chunk_causal_attention(q_nope, q_rope, k_nope, k_rope, v):
    B, S, H, _ = q_nope.shape
    nqb = S // Q_BLOCK
    scale = (QK_NOPE_DIM + QK_ROPE_DIM) ** -0.5
    k_chunk = jnp.arange(S) // CHUNK

    def to_blocks(t):
        return jnp.moveaxis(t.reshape(B, nqb, Q_BLOCK, *t.shape[2:]), 1, 0)

    def one_block(args):
        idx, qn, qr = args
        s = (jnp.einsum('bqhd,bkhd->bhqk', qn, k_nope)
             + jnp.einsum('bqhd,bkd->bhqk', qr, k_rope)).astype(jnp.float32) * scale
        q_chunk = (idx * Q_BLOCK + jnp.arange(Q_BLOCK)) // CHUNK
        mask = k_chunk[None, :] <= q_chunk[:, None]
        p = jax.nn.softmax(jnp.where(mask, s, -1e30), axis=-1).astype(v.dtype)
        return jnp.einsum('bhqk,bkhd->bqhd', p, v)

    out = lax.map(one_block, (jnp.arange(nqb), to_blocks(q_nope), to_blocks(q_rope)))
    return jnp.moveaxis(out, 0, 1).reshape(B, S, H, V_HEAD_DIM)


def mla_mixer(h, w_in, q_norm_g, kv_norm_g, w_uq, w_ukv, w_out):
    B, S, _ = h.shape
    o1 = Q_LORA_RANK
    o2 = o1 + KV_LORA_RANK
    o3 = o2 + QK_ROPE_DIM
    cq, ckv, k_rope, z = jnp.split(h @ w_in, [o1, o2, o3], axis=-1)
    q = (rms_norm(cq, q_norm_g) @ w_uq).reshape(B, S, MLA_HEADS, QK_NOPE_DIM + QK_ROPE_DIM)
    kv = (rms_norm(ckv, kv_norm_g) @ w_ukv).reshape(B, S, MLA_HEADS, QK_NOPE_DIM + V_HEAD_DIM)
    q_nope, q_rope = q[..., :QK_NOPE_DIM], q[..., QK_NOPE_DIM:]
    k_nope, v = kv[..., :QK_NOPE_DIM], kv[..., QK_NOPE_DIM:]
    pos = jnp.arange(S, dtype=jnp.float32)
    inv_freq = ROPE_THETA ** (-jnp.arange(0, QK_ROPE_DIM, 2, dtype=jnp.float32) / QK_ROPE_DIM)
    ang = pos[:, None] * inv_freq[None, :]
    cos, sin = jnp.cos(ang), jnp.sin(ang)
    q_rope = apply_rope(q_rope, cos[:, None, :], sin[:, None, :])
    k_rope = apply_rope(k_rope, cos, sin)
    o = chunk_causal_attention(q_nope, q_rope, k_nope, k_rope, v)
    y = o.reshape(B, S, MLA_WIDTH) * jax.nn.silu(z)
    return y @ w_out


def setup_inputs(seed: int = 0) -> dict:
    key = jax.random.key(seed)
    ks = jax.random.split(key, 17)
    f32 = jnp.float32
    D = D_MODEL

    def nrm(k, shape, s):
        return jax.random.normal(k, shape, f32) * s

    def gain(k, shape):
        return 1.0 + 0.05 * jax.random.normal(k, shape, f32)

    mla_in_cols = Q_LORA_RANK + KV_LORA_RANK + QK_ROPE_DIM + MLA_WIDTH
    return {
        'x': nrm(ks[0], (BATCH, SEQ, D), 1.0),
        'c': nrm(ks[1], (BATCH, D), 1.0),
        'ada_w': nrm(ks[2], (DEPTH, D, 3 * D), 0.5 * D ** -0.5),
        'ada_b': nrm(ks[3], (DEPTH, 3 * D), 0.01),
        'pre_g': gain(ks[4], (DEPTH, D)),
        'post_g': gain(ks[5], (DEPTH, D)),
        'sgu_w_in': nrm(ks[6], (N_A, D, 3 * SGU_WIDTH), D ** -0.5),
        'sgu_norm_g': gain(ks[7], (N_A, SGU_WIDTH)),
        'sgu_w_s': nrm(ks[8], (N_A, SGU_GROUPS, SGU_BLOCK, SGU_BLOCK), SGU_BLOCK ** -0.5),
        'sgu_b_s': gain(ks[9], (N_A, SGU_GROUPS, SGU_BLOCK)),
        'sgu_w_out': nrm(ks[10], (N_A, SGU_WIDTH, D), SGU_WIDTH ** -0.5),
        'mla_w_in': nrm(ks[11], (N_B, D, mla_in_cols), D ** -0.5),
        'mla_q_norm_g': gain(ks[12], (N_B, Q_LORA_RANK)),
        'mla_kv_norm_g': gain(ks[13], (N_B, KV_LORA_RANK)),
        'mla_w_uq': nrm(ks[14], (N_B, Q_LORA_RANK, MLA_HEADS * (QK_NOPE_DIM + QK_ROPE_DIM)), Q_LORA_RANK ** -0.5),
        'mla_w_ukv': nrm(ks[15], (N_B, KV_LORA_RANK, MLA_HEADS * (QK_NOPE_DIM + V_HEAD_DIM)), KV_LORA_RANK ** -0.5),
        'mla_w_out': nrm(ks[16], (N_B, MLA_WIDTH, D), MLA_WIDTH ** -0.5),
    }


def reference(x, c, ada_w, ada_b, pre_g, post_g, sgu_w_in, sgu_norm_g, sgu_w_s, sgu_b_s, sgu_w_out,
              mla_w_in, mla_q_norm_g, mla_kv_norm_g, mla_w_uq, mla_w_ukv, mla_w_out):
    cond = jax.nn.silu(c)
    for i in range(DEPTH):
        mod = cond @ ada_w[i] + ada_b[i]
        shift, scale, gate = jnp.split(mod, 3, axis=-1)
        h = rms_norm(x, pre_g[i]) * (1 + scale[:, None, :]) + shift[:, None, :]
        j = i // N_MIXERS
        if i % N_MIXERS == 0:
            y = sgu_mixer(h, sgu_w_in[j], sgu_norm_g[j], sgu_w_s[j], sgu_b_s[j], sgu_w_out[j])
        else:
            y = mla_mixer(h, mla_w_in[j], mla_q_norm_g[j], mla_kv_norm_g[j], mla_w_uq[j], mla_w_ukv[j], mla_w_out[j])
        x = x + gate[:, None, :] * rms_norm(y, post_g[i])
    return x
```

```python
import math
import numpy as np
from contextlib import ExitStack
import concourse.bass as bass
import concourse.mybir as mybir
from concourse.bass_utils import run_bass_kernel_spmd

F32 = mybir.dt.float32
BF16 = mybir.dt.bfloat16
AF = mybir.ActivationFunctionType
ALU = mybir.AluOpType
AX = mybir.AxisListType

D = 2048
E = 4096
H = 16
QR = 448
KVR = 512
EPS = 1e-6
SM_SCALE = 192.0 ** -0.5
ENGS = ("pe", "act", "dve", "pool", "sp")
NW = 3


class Op:
    __slots__ = ("eng", "fn", "waits", "signal", "tick", "lane", "idx", "is_dma")


class Prog:
    def __init__(self):
        self.ops = {e: [] for e in ENGS}
        self.lw = {}
        self.rd = {}
        self.lane_tick = {}
        self.const = set()

    def add(self, eng, fn, reads=(), writes=(), lane=None):
        op = Op()
        op.eng = eng
        op.fn = fn
        op.signal = False
        op.tick = None
        op.lane = lane
        op.is_dma = lane is not None
        op.idx = len(self.ops[eng])
        op.waits = {}
        deps = []
        for k in reads:
            w = self.lw.get(k)
            if w is not None:
                deps.append(w)
        for k in writes:
            w = self.lw.get(k)
            if w is not None:
                deps.append(w)
            deps.extend(self.rd.get(k, ()))
        for a in deps:
            if a is op:
                continue
            if a.is_dma:
                key = ("l", a.lane)
                op.waits[key] = max(op.waits.get(key, 0), self.lane_tick[a.lane])
            else:
                if a.eng == eng and not op.is_dma:
                    if eng == "pe":
                        continue
                    if op.idx - a.idx > 2:
                        continue
                a.signal = True
                key = ("e", a.eng)
                prev = op.waits.get(key)
                if prev is None or prev.idx < a.idx:
                    op.waits[key] = a
        if op.is_dma:
            self.lane_tick[lane] = self.lane_tick.get(lane, 0) + 16
            op.tick = self.lane_tick[lane]
        for k in reads:
            if k not in self.const:
                self.rd.setdefault(k, []).append(op)
        for k in writes:
            self.lw[k] = op
            self.rd[k] = []
        self.ops[eng].append(op)
        return op

    def barrier(self):
        lasts = {}
        for e in ENGS:
            for a in reversed(self.ops[e]):
                if not a.is_dma and a.fn is not None:
                    lasts[e] = a
                    break
        lanes = dict(self.lane_tick)
        for e in ENGS:
            op = Op()
            op.eng = e
            op.fn = None
            op.signal = False
            op.tick = None
            op.lane = None
            op.is_dma = False
            op.idx = len(self.ops[e])
            op.waits = {}
            for x, a in lasts.items():
                a.signal = True
                op.waits[("e", x)] = a
            for ln, v in lanes.items():
                op.waits[("l", ln)] = v
            self.ops[e].append(op)
        self.lw = {}
        self.rd = {}

    def finalize(self):
        for e in ENGS:
            t = 0
            for op in self.ops[e]:
                if (not op.is_dma) and op.signal:
                    t += 1
                    op.tick = t

    def emit(self, engname, eng, esem, lsem):
        waited = {}
        for op in self.ops[engname]:
            for key, v in op.waits.items():
                if key[0] == "e":
                    sem = esem[key[1]]
                    val = v.tick
                else:
                    sem = lsem[key[1]]
                    val = v
                if waited.get(key, 0) >= val:
                    continue
                waited[key] = val
                eng.wait_ge(sem, val)
            if op.fn is None:
                continue
            ins = op.fn(eng)
            if op.is_dma:
                ins.then_inc(lsem[op.lane], 16)
            elif op.signal:
                ins.then_inc(esem[engname], 1)


class Arena:
    def __init__(self, t, size):
        self.t = t
        self.size = size
        self.off = 0

    def alloc(self, free_shape, dtype):
        n = 1
        for s in free_shape:
            n *= s
        ne = n * (2 if dtype == F32 else 1)
        self.off = (self.off + 31) // 32 * 32
        a = self.t[:, self.off:self.off + ne]
        self.off += ne
        assert self.off <= self.size, f"arena overflow {self.off} > {self.size}"
        if dtype == F32:
            a = a.bitcast(F32)
        if len(free_shape) == 2:
            a = a.rearrange("p (a b) -> p a b", a=free_shape[0])
        elif len(free_shape) == 3:
            a = a.rearrange("p (a b c) -> p a b c", a=free_shape[0], b=free_shape[1])
        return a


def build(S, depth, debug=False):
    NT = S // 512
    NB = S // 128
    nc = bass.Bass("TRN2", target_bir_lowering=False)

    def din(name, shape):
        return nc.dram_tensor(name, list(shape), F32, kind="ExternalInput").ap()

    x_d = din("x", [S, D])
    cT_d = din("cT", [128, 16])
    ada_w = din("ada_w", [4, D, 3 * D])
    ada_b = din("ada_b", [4, 3 * D])
    pre_g = din("pre_g", [4, D])
    post_g = din("post_g", [4, D])
    sgu_w_in = din("sgu_w_in", [2, D, 3 * E])
    sgu_gT = din("sgu_gT", [2, 128, 32])
    sgu_w_s = din("sgu_w_s", [2, 16, 128, 128])
    sgu_b_s = din("sgu_b_s", [2, 16 * 128])
    sgu_w_out = din("sgu_w_out", [2, E, D])
    mla_w_in = din("mla_w_in", [2, D, 3072])
    mla_qg = din("mla_qg", [2, QR])
    mla_kvg = din("mla_kvg", [2, KVR])
    mla_w_uq = din("mla_w_uq", [2, QR, 3072])
    mla_w_ukv = din("mla_w_ukv", [2, KVR, 4096])
    mla_w_out = din("mla_w_out", [2, D, D])
    ident_d = din("ident", [128, 128])
    mask_d = din("mask", [128, 128])
    cos_d = din("cos_tm", [128, (S // 128) * 32])
    sin_d = din("sin_tm", [128, (S // 128) * 32])
    ttab_d = din("ttab", [128, S])
    y_d = nc.dram_tensor("y", [S, D], F32, kind="ExternalOutput").ap()
    ao_d = nc.dram_tensor("attn_o", [S, D], BF16, kind="ExternalOutput" if debug else "Internal").ap()
    if debug:
        y0_d = nc.dram_tensor("y0", [S, D], F32, kind="ExternalOutput").ap()
        lat_d = nc.dram_tensor("lat", [128, 9 * S], BF16, kind="ExternalOutput").ap()

    P = Prog()
    ARENA = 106400
    with ExitStack() as ctx:
        arena_t = ctx.enter_context(nc.sbuf_tensor("arena", [128, ARENA], BF16))
        A = Arena(arena_t, ARENA)
        PS = [ctx.enter_context(nc.psum_tensor(f"ps{i}", [128, 512], F32)) for i in range(8)]
        psn = [0]

        def ps_next():
            i = psn[0] % 8
            psn[0] += 1
            return PS[i], ("ps", i)

        identf = A.alloc([128], F32)
        identb = A.alloc([128], BF16)
        maskt = A.alloc([128], F32)
        condT = A.alloc([16], F32)
        cond_bc = A.alloc([16, 128], BF16)
        eps_t = A.alloc([1], F32)
        A_bc = A.alloc([D], F32)
        shift_bc = A.alloc([D], F32)
        G_bc = A.alloc([D], F32)
        small_ap = A.alloc([1024], F32)

        class Small:
            off = 0

            @staticmethod
            def alloc(shape):
                n = 1
                for d_ in shape:
                    n *= d_
                v = small_ap[:, Small.off:Small.off + n]
                Small.off += n
                assert Small.off <= 1024
                if len(shape) == 2:
                    v = v.rearrange("p (a b) -> p a b", a=shape[0])
                elif len(shape) == 3:
                    v = v.rearrange("p (a b c) -> p a b c", a=shape[0], b=shape[1])
                return v
        Wring = [A.alloc([16, 512], BF16) for _ in range(NW)]
        wcnt = [0]
        m_persist = A.off
        for k in ("identf", "identb", "maskt", "cond_bc", "eps"):
            P.const.add(k)

        def wslot():
            s = wcnt[0] % NW
            wcnt[0] += 1
            return s

        def wload(src, kp=128):
            s = wslot()
            K, N = src.shape
            kc = K // kp
            view = Wring[s][0:kp, 0:kc, 0:N]
            srcv = src.rearrange("(k p) n -> p k n", p=kp)
            P.add("pool", lambda e, o=view, i=srcv: e.dma_start(out=o, in_=i),
                  writes=[("W", s)], lane=("W", s))
            return view, ("W", s)

        def dma(eng, out, in_, reads, writes, lane):
            P.add(eng, lambda e, o=out, i=in_: e.dma_start(out=o, in_=i), reads=reads, writes=writes, lane=lane)

        def mm_group(ps_ap, pskey, pairs, reads):
            def fn(e, ps_ap=ps_ap, pairs=pairs):
                n = len(pairs)
                ins = None
                for i, (l, r) in enumerate(pairs):
                    ins = e.matmul(ps_ap, l, r, start=(i == 0), stop=(i == n - 1))
                return ins
            P.add("pe", fn, reads=reads, writes=[pskey])

        def transposes(ps_ap_list, pskey, srcs, ident, reads):
            def fn(e, outs=ps_ap_list, srcs=srcs, ident=ident):
                ins = None
                for o, s_ in zip(outs, srcs):
                    ins = e.transpose(o, s_, ident)
                return ins
            P.add("pe", fn, reads=reads, writes=[pskey])

        evac_rr = [0]

        def evac_copy(out, in_, reads, writes, eng=None):
            evac_rr[0] += 1
            if eng != "dve" and evac_rr[0] % 2 == 0:
                P.add("act", lambda e, o=out, i=in_: e.activation(out=o, in_=i, func=AF.Copy), reads=reads, writes=writes)
            else:
                P.add("dve", lambda e, o=out, i=in_: e.tensor_copy(out=o, in_=i), reads=reads, writes=writes)

        dma("sp", identf, ident_d, [], ["identf"], "c")
        dma("sp", maskt, mask_d, [], ["maskt"], "c")
        dma("sp", condT, cT_d, [], ["condT"], "c")
        P.add("dve", lambda e: e.tensor_copy(out=identb, in_=identf), reads=["identf"], writes=["identb"])
        P.add("dve", lambda e: e.memset(eps_t, EPS), writes=["eps"])
        sig_t = A.alloc([16], F32)
        P.add("act", lambda e: e.activation(out=sig_t, in_=condT, func=AF.Silu), reads=["condT"], writes=["sig"])
        P.add("dve", lambda e: e.tensor_copy(out=cond_bc, in_=sig_t.unsqueeze(2).to_broadcast([128, 16, 128])),
              reads=["sig"], writes=["cond_bc"])
        m_persist = A.off

        def ada_mod(i):
            A.off = m_persist
            m0 = A.off
            btmp = [A.alloc([512], F32) for _ in range(2)]
            gtmp = [A.alloc([512], F32) for _ in range(2)]
            tmp = [A.alloc([512], F32) for _ in range(2)]
            for n in range(12):
                q = n % 2
                W, wk = wload(ada_w[i, :, n * 512:(n + 1) * 512])
                dma("sp", btmp[q], ada_b[i:i + 1, n * 512:(n + 1) * 512].to_broadcast([128, 512]), [], [("btmp", q)], ("sm", q))
                if 4 <= n < 8:
                    dma("sp", gtmp[q], pre_g[i:i + 1, (n - 4) * 512:(n - 3) * 512].to_broadcast([128, 512]), [], [("gtmp", q)], ("sm", q))
                elif n >= 8:
                    dma("sp", gtmp[q], post_g[i:i + 1, (n - 8) * 512:(n - 7) * 512].to_broadcast([128, 512]), [], [("gtmp", q)], ("sm", q))
                ps, pk = ps_next()
                mm_group(ps[:, :], pk, [(cond_bc[:, k, :], W[:, k, :]) for k in range(16)], ["cond_bc", wk])
                if n < 4:
                    o = shift_bc[:, n * 512:(n + 1) * 512]
                    P.add("dve", lambda e, o=o, ps=ps, b=btmp[q]: e.tensor_tensor(out=o, in0=ps[:, :], in1=b, op=ALU.add),
                          reads=[pk, ("btmp", q)], writes=["shift"])
                elif n < 8:
                    o = A_bc[:, (n - 4) * 512:(n - 3) * 512]
                    P.add("dve", lambda e, t=tmp[q], ps=ps, b=btmp[q]: e.tensor_tensor(out=t, in0=ps[:, :], in1=b, op=ALU.add),
                          reads=[pk, ("btmp", q)], writes=[("adt", q)])
                    P.add("dve", lambda e, o=o, t=tmp[q], g=gtmp[q]: e.scalar_tensor_tensor(out=o, in0=t, scalar=1.0, in1=g, op0=ALU.add, op1=ALU.mult),
                          reads=[("adt", q), ("gtmp", q)], writes=["A"])
                else:
                    o = G_bc[:, (n - 8) * 512:(n - 7) * 512]
                    P.add("dve", lambda e, t=tmp[q], ps=ps, b=btmp[q]: e.tensor_tensor(out=t, in0=ps[:, :], in1=b, op=ALU.add),
                          reads=[pk, ("btmp", q)], writes=[("adt", q)])
                    P.add("dve", lambda e, o=o, t=tmp[q], g=gtmp[q]: e.tensor_tensor(out=o, in0=t, in1=g, op=ALU.mult),
                          reads=[("adt", q), ("gtmp", q)], writes=["G"])
            P.barrier()
            A.off = m0

        class NBuf:
            pass

        def alloc_nstage():
            nb = NBuf()
            nb.xblk = [A.alloc([D], F32) for _ in range(2)]
            nb.hb = A.alloc([D], BF16)
            nb.hT = A.alloc([16, 512], BF16)
            Small.off = 0
            nb.ss = Small.alloc([8])
            nb.std = Small.alloc([8])
            nb.rstd = Small.alloc([8])
            nb.sso = Small.alloc([4, 4])
            nb.sst = Small.alloc([4])
            nb.stdo = Small.alloc([4])
            nb.rstdo = Small.alloc([4])
            nb.xc = 0
            return nb

        def rstd_from_ss(ss_ap, std_ap, rstd_ap, inv_n, kss, kstd, krstd):
            P.add("act", lambda e: e.activation(out=std_ap, in_=ss_ap, func=AF.Sqrt, bias=eps_t[:, 0:1], scale=inv_n),
                  reads=[kss, "eps"], writes=[kstd])
            P.add("dve", lambda e: e.reciprocal(out=rstd_ap, in_=std_ap), reads=[kstd], writes=[krstd])

        def stage_n(nb, src, t):
            for b in range(4):
                r0 = (t * 4 + b) * 128
                xs = nb.xc % 2
                nb.xc += 1
                xb = nb.xblk[xs]
                dma("sp", xb, src[r0:r0 + 128, :], [("dram", t * 4 + b)], [("xblk", xs)], ("x", xs))
                ssb = nb.ss[:, b:b + 1]
                P.add("dve", lambda e, o=ssb: e.memset(o, 0.0), writes=[("ss", b)])
                P.add("act", lambda e, xb=xb, o=ssb: e.activation(out=nb.hb, in_=xb, func=AF.Square, accum_out=o),
                      reads=[("xblk", xs), ("ss", b)], writes=["hb", ("ss", b)])
                rstd_from_ss(ssb, nb.std[:, b:b + 1], nb.rstd[:, b:b + 1], 1.0 / D, ("ss", b), ("std", b), ("rstd", b))
                P.add("dve", lambda e, xb=xb, r=nb.rstd[:, b:b + 1]: e.scalar_tensor_tensor(out=xb, in0=xb, scalar=r, in1=A_bc, op0=ALU.mult, op1=ALU.mult),
                      reads=[("xblk", xs), ("rstd", b), "A"], writes=[("xblk", xs)])
                P.add("dve", lambda e, xb=xb: e.tensor_tensor(out=nb.hb, in0=xb, in1=shift_bc, op=ALU.add),
                      reads=[("xblk", xs), "shift"], writes=["hb"])
                for kq in range(4):
                    ps, pk = ps_next()
                    psb = ps[:, :].bitcast(BF16)
                    transposes([psb[:, q * 128:(q + 1) * 128] for q in range(4)], pk,
                               [nb.hb[:, (kq * 4 + q) * 128:(kq * 4 + q + 1) * 128] for q in range(4)], identb, ["hb", "identb"])
                    evac_copy(nb.hT[:, kq * 4:(kq + 1) * 4, b * 128:(b + 1) * 128],
                              psb[:, 0:512].rearrange("p (a b) -> p a b", a=4), [pk], ["hT"])

        def out_stage(nb, yT, nk, wsrc, src, dst, t, ob, obkey, junk):
            sso, sst, stdo, rstdo = nb.sso, nb.sst, nb.stdo, nb.rstdo
            P.add("dve", lambda e: e.memset(sso, 0.0), writes=["sso"])
            for n in range(4):
                Ws = []
                for kh in range(nk // 16):
                    Ws.append(wload(wsrc[kh * 2048:(kh + 1) * 2048, n * 512:(n + 1) * 512]))
                for b in range(4):
                    ps, pk = ps_next()
                    pairs = [(yT[:, k, b * 128:(b + 1) * 128], Ws[k // 16][0][:, k % 16, :]) for k in range(nk)]
                    mm_group(ps[:, :], pk, pairs, ["yT"] + [w[1] for w in Ws])
                    P.add("act", lambda e, o=ob[:, b, n * 512:(n + 1) * 512], ps=ps: e.activation(out=o, in_=ps[:, :], func=AF.Copy),
                          reads=[pk], writes=[(obkey, b)])
                    P.add("act", lambda e, ps=ps, o=sso[:, b, n:n + 1]: e.activation(out=junk, in_=ps[:, :], func=AF.Square, accum_out=o),
                          reads=[pk, "sso"], writes=["hb", "sso"])
            for b in range(4):
                r0 = (t * 4 + b) * 128
                P.add("dve", lambda e, o=sst[:, b:b + 1], i=sso[:, b, :]: e.reduce_sum(out=o, in_=i, axis=AX.X),
                      reads=["sso"], writes=[("sst", b)])
                rstd_from_ss(sst[:, b:b + 1], stdo[:, b:b + 1], rstdo[:, b:b + 1], 1.0 / D, ("sst", b), ("stdo", b), ("rstdo", b))
                xs = nb.xc % 2
                nb.xc += 1
                xb = nb.xblk[xs]
                dma("sp", xb, src[r0:r0 + 128, :], [("dram", t * 4 + b)], [("xblk", xs)], ("x", xs))
                P.add("dve", lambda e, o=ob[:, b, :], r=rstdo[:, b:b + 1]: e.scalar_tensor_tensor(out=o, in0=o, scalar=r, in1=G_bc, op0=ALU.mult, op1=ALU.mult),
                      reads=[(obkey, b), ("rstdo", b), "G"], writes=[(obkey, b)])
                P.add("dve", lambda e, o=ob[:, b, :], xb=xb: e.tensor_tensor(out=o, in0=o, in1=xb, op=ALU.add),
                      reads=[(obkey, b), ("xblk", xs)], writes=[(obkey, b)])
                dma("sp", dst[r0:r0 + 128, :], ob[:, b, :], [(obkey, b)], [("dram", t * 4 + b)], ("out", b % 2))

        def sgu_layer(j, src, dst):
            A.off = m_persist
            nb = alloc_nstage()
            gv = A.alloc([4, E], BF16)
            ob = gv.rearrange("p a b -> p (a b)").bitcast(F32).rearrange("p (a b) -> p a b", a=4)
            wTb = A.alloc([16, 128], BF16)
            bs_bc = A.alloc([16, 128], F32)
            gammaT = Small.alloc([32])
            gu = [A.alloc([512], F32) for _ in range(2)]
            sz = [A.alloc([512], F32) for _ in range(2)]
            t1 = [A.alloc([512], F32) for _ in range(2)]
            stats = Small.alloc([4, 8, 6])
            mv = Small.alloc([4, 2])
            stdv = Small.alloc([4])
            rstdv = Small.alloc([4])
            junk = nb.hb.bitcast(F32)[:, 0:512]
            yT = A.alloc([32, 512], BF16)
            wsl = yT.rearrange("p a b -> p (a b)")[:, 0:4096].bitcast(F32).rearrange("p (a b) -> p a b", a=16)
            dma("sp", wsl, sgu_w_s[j].rearrange("g t s -> t g s"), [], ["yT"], "c")
            dma("sp", bs_bc.rearrange("p a b -> p (a b)"), sgu_b_s[j:j + 1, :].to_broadcast([128, 2048]), [], ["bs"], "c")
            dma("sp", gammaT, sgu_gT[j], [], ["gammaT"], "c")
            P.add("dve", lambda e: e.tensor_tensor(out=wsl, in0=wsl, in1=maskt.unsqueeze(1).to_broadcast([128, 16, 128]), op=ALU.mult),
                  reads=["yT", "maskt"], writes=["yT"])
            for gq in range(4):
                ps, pk = ps_next()
                transposes([ps[:, q * 128:(q + 1) * 128] for q in range(4)], pk,
                           [wsl[:, gq * 4 + q, :] for q in range(4)], identf, ["yT", "identf"])
                evac_copy(wTb[:, gq * 4:(gq + 1) * 4, :], ps[:, :].rearrange("p (a b) -> p a b", a=4), [pk], ["wTb"])
            for t in range(NT):
                stage_n(nb, src, t)
                for n in range(8):
                    W, wk = wload(sgu_w_in[j, :, E + n * 512:E + (n + 1) * 512])
                    for b in range(4):
                        ps, pk = ps_next()
                        mm_group(ps[:, :], pk, [(nb.hT[:, k, b * 128:(b + 1) * 128], W[:, k, :]) for k in range(16)], ["hT", wk])
                        gvs = gv[:, b, n * 512:(n + 1) * 512]
                        P.add("act", lambda e, o=gvs, ps=ps: e.activation(out=o, in_=ps[:, :], func=AF.Gelu),
                              reads=[pk], writes=[("gv", b)])
                        P.add("dve", lambda e, o=stats[:, b, n, :], i=gvs: e.bn_stats(out=o, in_=i),
                              reads=[("gv", b)], writes=[("stats", b)])
                for b in range(4):
                    P.add("dve", lambda e, o=mv[:, b, :], i=stats[:, b, :, :]: e.bn_aggr(out=o, in_=i.rearrange("p a b -> p (a b)")),
                          reads=[("stats", b)], writes=[("mv", b)])
                    rstd_from_ss(mv[:, b, 1:2], stdv[:, b:b + 1], rstdv[:, b:b + 1], 1.0, ("mv", b), ("stdv", b), ("rstdv", b))
                    P.add("dve", lambda e, o=gv[:, b, :], m=mv[:, b, 0:1], r=rstdv[:, b:b + 1]: e.tensor_scalar(out=o, in0=o, scalar1=m, scalar2=r, op0=ALU.subtract, op1=ALU.mult),
                          reads=[("gv", b), ("mv", b), ("rstdv", b)], writes=[("gv", b)])
                for cg in range(8):
                    Wu, wuk = wload(sgu_w_in[j, :, cg * 512:(cg + 1) * 512])
                    Wz, wzk = wload(sgu_w_in[j, :, 2 * E + cg * 512:2 * E + (cg + 1) * 512])
                    for q in range(4):
                        ch = cg * 4 + q
                        g = ch // 2
                        bq = ch % 2
                        psU, pkU = ps_next()
                        mm_group(psU[:, :], pkU, [(Wu[:, k, q * 128:(q + 1) * 128], nb.hT[:, k, :]) for k in range(16)], ["hT", wuk])
                        psZ, pkZ = ps_next()
                        mm_group(psZ[:, :], pkZ, [(Wz[:, k, q * 128:(q + 1) * 128], nb.hT[:, k, :]) for k in range(16)], ["hT", wzk])
                        psM, pkM = ps_next()

                        def mixfn(e, psM=psM, ch=ch, g=g):
                            ins = None
                            for b in range(4):
                                ins = e.matmul(psM[:, b * 128:(b + 1) * 128], gv[:, b, ch * 128:(ch + 1) * 128], wTb[:, g, :], start=True, stop=True)
                            return ins
                        P.add("pe", mixfn, reads=[("gv", 0), ("gv", 1), ("gv", 2), ("gv", 3), "wTb"], writes=[pkM])
                        P.add("act", lambda e, o=gu[bq], ps=psU: e.activation(out=o, in_=ps[:, :], func=AF.Gelu), reads=[pkU], writes=[("gu", bq)])
                        P.add("act", lambda e, o=sz[bq], ps=psZ: e.activation(out=o, in_=ps[:, :], func=AF.Silu), reads=[pkZ], writes=[("sz", bq)])
                        t1v = t1[bq].rearrange("p (a b) -> p a b", a=4)
                        P.add("dve", lambda e, o=t1v, psM=psM, ch=ch, g=g: e.scalar_tensor_tensor(
                            out=o, in0=psM[:, :].rearrange("p (a b) -> p a b", a=4), scalar=gammaT[:, ch:ch + 1],
                            in1=bs_bc[:, g, :].unsqueeze(1).to_broadcast([128, 4, 128]), op0=ALU.mult, op1=ALU.add),
                            reads=[pkM, "gammaT", "bs"], writes=[("t1", bq)])
                        P.add("dve", lambda e, o=t1[bq], a=gu[bq]: e.tensor_tensor(out=o, in0=o, in1=a, op=ALU.mult),
                              reads=[("t1", bq), ("gu", bq)], writes=[("t1", bq)])
                        P.add("dve", lambda e, o=yT[:, ch, :], a=t1[bq], b_=sz[bq]: e.tensor_tensor(out=o, in0=a, in1=b_, op=ALU.mult),
                              reads=[("t1", bq), ("sz", bq)], writes=["yT"])
                out_stage(nb, yT, 32, sgu_w_out[j], src, dst, t, ob, "gv", junk)
            P.barrier()

        def mla_layer(j, src, dst):
            A.off = m_persist
            LAT_OFF = (ARENA - (9 * S + 96)) // 32 * 32
            nb = alloc_nstage()
            cos_t = [A.alloc([4, 32], F32) for _ in range(2)]
            sin_t = [A.alloc([4, 32], F32) for _ in range(2)]
            gq_bc = A.alloc([QR], F32)
            gkv_bc = A.alloc([KVR], F32)
            cqn_b = A.alloc([QR], BF16)
            ckvn_b = A.alloc([KVR], BF16)
            kr2b = A.alloc([128], BF16)
            rt = [A.alloc([32], F32) for _ in range(4)]
            junk = nb.hb.bitcast(F32)[:, 0:512]
            ssq = Small.alloc([2])
            stdq = Small.alloc([2])
            rq = Small.alloc([2])
            assert A.off <= LAT_OFF, (A.off, LAT_OFF)
            A.off = LAT_OFF
            cqT = A.alloc([4, S], BF16)
            ckvT = A.alloc([4, S], BF16)
            Kr2T = A.alloc([S], BF16)
            dma("sp", gq_bc, mla_qg[j:j + 1, :].to_broadcast([128, QR]), [], ["gq"], "c")
            dma("sp", gkv_bc, mla_kvg[j:j + 1, :].to_broadcast([128, KVR]), [], ["gkv"], "c")
            for t in range(NT):
                cv = t % 2
                dma("sp", cos_t[cv].rearrange("p a b -> p (a b)"), cos_d[:, t * 128:(t + 1) * 128], [], [("cos", cv)], ("cs", cv))
                dma("sp", sin_t[cv].rearrange("p a b -> p (a b)"), sin_d[:, t * 128:(t + 1) * 128], [], [("sin", cv)], ("cs", cv))
                stage_n(nb, src, t)
                Wa, wak = wload(mla_w_in[j, :, 0:QR])
                Wb, wbk = wload(mla_w_in[j, :, QR:QR + KVR])
                Wc, wck = wload(mla_w_in[j, :, QR + KVR:QR + KVR + 64])
                for b in range(4):
                    blk = t * 4 + b
                    psA, pkA = ps_next()
                    mm_group(psA[:, 0:QR], pkA, [(nb.hT[:, k, b * 128:(b + 1) * 128], Wa[:, k, :]) for k in range(16)], ["hT", wak])
                    psB, pkB = ps_next()
                    mm_group(psB[:, :], pkB, [(nb.hT[:, k, b * 128:(b + 1) * 128], Wb[:, k, :]) for k in range(16)], ["hT", wbk])
                    psC, pkC = ps_next()
                    mm_group(psC[:, 0:64], pkC, [(nb.hT[:, k, b * 128:(b + 1) * 128], Wc[:, k, :]) for k in range(16)], ["hT", wck])
                    for (ps_, pk_, n_, g_, gk_, dstb, dk, ci) in ((psA, pkA, QR, gq_bc, "gq", cqn_b, "cqn", 0), (psB, pkB, KVR, gkv_bc, "gkv", ckvn_b, "ckvn", 1)):
                        P.add("dve", lambda e, o=ssq[:, ci:ci + 1]: e.memset(o, 0.0), writes=[("ssq", ci)])
                        P.add("act", lambda e, ps_=ps_, n_=n_, o=ssq[:, ci:ci + 1]: e.activation(out=junk[:, 0:n_], in_=ps_[:, 0:n_], func=AF.Square, accum_out=o),
                              reads=[pk_, ("ssq", ci)], writes=["hb", ("ssq", ci)])
                        rstd_from_ss(ssq[:, ci:ci + 1], stdq[:, ci:ci + 1], rq[:, ci:ci + 1], 1.0 / n_, ("ssq", ci), ("stdq", ci), ("rq", ci))
                        P.add("dve", lambda e, ps_=ps_, n_=n_, g_=g_, o=dstb, r=rq[:, ci:ci + 1]: e.scalar_tensor_tensor(
                            out=o, in0=ps_[:, 0:n_], scalar=r, in1=g_, op0=ALU.mult, op1=ALU.mult),
                            reads=[pk_, ("rq", ci), gk_], writes=[dk])
                    ps, pk = ps_next()
                    psb = ps[:, :].bitcast(BF16)
                    transposes([psb[0:112, q * 128:(q + 1) * 128] for q in range(4)], pk,
                               [cqn_b[:, q * 112:(q + 1) * 112] for q in range(4)], identb, ["cqn", "identb"])
                    evac_copy(cqT[0:112, :, blk * 128:(blk + 1) * 128], psb[0:112, 0:512].rearrange("p (a b) -> p a b", a=4), [pk], ["cqT"], eng="dve")
                    ps, pk = ps_next()
                    psb = ps[:, :].bitcast(BF16)
                    transposes([psb[:, q * 128:(q + 1) * 128] for q in range(4)], pk,
                               [ckvn_b[:, q * 128:(q + 1) * 128] for q in range(4)], identb, ["ckvn", "identb"])
                    evac_copy(ckvT[:, :, blk * 128:(blk + 1) * 128], psb[:, 0:512].rearrange("p (a b) -> p a b", a=4), [pk], ["ckvT"], eng="dve")
                    k1 = psC[:, 0:32]
                    k2 = psC[:, 32:64]
                    cs = cos_t[cv][:, b, :]
                    sn = sin_t[cv][:, b, :]
                    P.add("dve", lambda e, k1=k1, cs=cs: e.tensor_tensor(out=rt[0], in0=k1, in1=cs, op=ALU.mult), reads=[pkC, ("cos", cv)], writes=[("rt", 0)])
                    P.add("dve", lambda e, k2=k2, sn=sn: e.tensor_tensor(out=rt[1], in0=k2, in1=sn, op=ALU.mult), reads=[pkC, ("sin", cv)], writes=[("rt", 1)])
                    P.add("dve", lambda e, k2=k2, cs=cs: e.tensor_tensor(out=rt[2], in0=k2, in1=cs, op=ALU.mult), reads=[pkC, ("cos", cv)], writes=[("rt", 2)])
                    P.add("dve", lambda e, k1=k1, sn=sn: e.tensor_tensor(out=rt[3], in0=k1, in1=sn, op=ALU.mult), reads=[pkC, ("sin", cv)], writes=[("rt", 3)])
                    for c0 in (0, 64):
                        P.add("dve", lambda e, c0=c0: e.tensor_tensor(out=kr2b[:, c0:c0 + 32], in0=rt[0], in1=rt[1], op=ALU.subtract),
                              reads=[("rt", 0), ("rt", 1)], writes=["kr2b"])
                        P.add("dve", lambda e, c0=c0: e.tensor_tensor(out=kr2b[:, c0 + 32:c0 + 64], in0=rt[2], in1=rt[3], op=ALU.add),
                              reads=[("rt", 2), ("rt", 3)], writes=["kr2b"])
                    ps, pk = ps_next()
                    psb = ps[:, :].bitcast(BF16)
                    transposes([psb[:, 0:128]], pk, [kr2b[:, :]], identb, ["kr2b", "identb"])
                    evac_copy(Kr2T[:, blk * 128:(blk + 1) * 128], psb[:, 0:128], [pk], ["Kr2T"], eng="dve")
            P.barrier()
            if debug and j == 0:
                dma("sp", lat_d[0:112, 0:4 * S], cqT[0:112].rearrange("p a b -> p (a b)"), [], [], "dbg")
                dma("sp", lat_d[:, 4 * S:8 * S], ckvT.rearrange("p a b -> p (a b)"), [], [], "dbg")
                dma("sp", lat_d[:, 8 * S:9 * S], Kr2T, [], [], "dbg")
                P.barrier()
            A.off = m_persist
            KnT = [A.alloc([S], BF16) for _ in range(2)]
            Vh = [A.alloc([NB, 132], BF16) for _ in range(2)]
            Qn = [A.alloc([512], BF16) for _ in range(2)]
            Qr2 = [A.alloc([512], BF16) for _ in range(2)]
            PT = [A.alloc([512], BF16) for _ in range(3)]
            Ob = [A.alloc([4, 128], BF16) for _ in range(2)]
            Tt = [A.alloc([512], F32) for _ in range(2)]
            assert A.off <= LAT_OFF, (A.off, LAT_OFF)
            Small.off = 0
            rs = Small.alloc([8])
            for v in range(2):
                P.add("dve", lambda e, v=v: e.memset(Vh[v][:, :, 128:129], 1.0), writes=[("Vh", v)])
            sc_c = [0]
            pj_c = [0]
            ptc = [0]
            obc = [0]
            ttc = [0]
            for h in range(H):
                hv = h % 2
                Wk, wkk = wload(mla_w_ukv[j, :, h * 256:(h + 1) * 256])
                for tt in range(NT):
                    pi = 6 + pj_c[0] % 2
                    pj_c[0] += 1
                    mm_group(PS[pi][:, :], ("ps", pi), [(Wk[:, kc, 0:128], ckvT[:, kc, tt * 512:(tt + 1) * 512]) for kc in range(4)], [wkk, "ckvT"])
                    evac_copy(KnT[hv][:, tt * 512:(tt + 1) * 512], PS[pi][:, :], [("ps", pi)], [("KnT", hv)])
                for tt in range(NT):
                    pi = 6 + pj_c[0] % 2
                    pj_c[0] += 1

                    def vfn(e, pi=pi, tt=tt, Wk=Wk):
                        ins = None
                        for bl in range(4):
                            for kc in range(4):
                                ins = e.matmul(PS[pi][:, bl * 128:(bl + 1) * 128], ckvT[:, kc, (tt * 4 + bl) * 128:(tt * 4 + bl + 1) * 128],
                                               Wk[:, kc, 128:256], start=(kc == 0), stop=(kc == 3))
                        return ins
                    P.add("pe", vfn, reads=[wkk, "ckvT"], writes=[("ps", pi)])
                    evac_copy(Vh[hv][:, tt * 4:(tt + 1) * 4, 0:128], PS[pi][:, :].rearrange("p (a b) -> p a b", a=4), [("ps", pi)], [("Vh", hv)])
                s = wslot()
                Wq = Wring[s][0:112, 0:4, 0:256]
                base = h * 192
                srcq = mla_w_uq[j].rearrange("(k p) n -> p k n", p=112)
                P.add("pool", lambda e, o=Wq[:, :, 0:192], i=srcq[:, :, base:base + 192]: e.dma_start(out=o, in_=i), writes=[("W", s)], lane=("W", s))
                P.add("pool", lambda e, o=Wq[:, :, 192:224], i=srcq[:, :, base + 160:base + 192]: e.dma_start(out=o, in_=i), writes=[("W", s)], lane=("W", s))
                P.add("pool", lambda e, o=Wq[:, :, 224:256], i=srcq[:, :, base + 128:base + 160]: e.dma_start(out=o, in_=i), writes=[("W", s)], lane=("W", s))
                wqk = ("W", s)
                for jq in range(NT):
                    qv = jq % 2
                    tv = ttc[0] % 2
                    ttc[0] += 1
                    dma("sp", Tt[tv], ttab_d[:, jq * 512:(jq + 1) * 512], [], [("Tt", tv)], ("Tt", tv))
                    pi = 6 + pj_c[0] % 2
                    pj_c[0] += 1
                    mm_group(PS[pi][:, :], ("ps", pi), [(Wq[:, kc, 0:128], cqT[0:112, kc, jq * 512:(jq + 1) * 512]) for kc in range(4)], [wqk, "cqT"])
                    P.add("act", lambda e, o=Qn[qv], pi=pi: e.activation(out=o, in_=PS[pi][:, :], func=AF.Copy, scale=SM_SCALE),
                          reads=[("ps", pi)], writes=[("Qn", qv)])
                    pi2 = 6 + pj_c[0] % 2
                    pj_c[0] += 1
                    mm_group(PS[pi2][:, :], ("ps", pi2), [(Wq[:, kc, 128:256], cqT[0:112, kc, jq * 512:(jq + 1) * 512]) for kc in range(4)], [wqk, "cqT"])
                    P.add("dve", lambda e, o=Qr2[qv], pi2=pi2, tb=Tt[tv]: e.tensor_tensor(out=o, in0=PS[pi2][:, :], in1=tb, op=ALU.mult),
                          reads=[("ps", pi2), ("Tt", tv)], writes=[("Qr2", qv)])
                    nkb = 4 * (jq + 1)
                    for kb in range(nkb):
                        i_ = kb - 4 * jq
                        m0 = max(i_, 0)
                        c0 = m0 * 128
                        si = 4 + sc_c[0] % 2
                        sc_c[0] += 1

                        def sfn(e, si=si, c0=c0, kb=kb, hv=hv, qv=qv):
                            e.matmul(PS[si][:, c0:512], KnT[hv][:, kb * 128:(kb + 1) * 128], Qn[qv][:, c0:512], start=True, stop=False)
                            return e.matmul(PS[si][:, c0:512], Kr2T[:, kb * 128:(kb + 1) * 128], Qr2[qv][:, c0:512], start=False, stop=True)
                        P.add("pe", sfn, reads=[("KnT", hv), "Kr2T", ("Qn", qv), ("Qr2", qv)], writes=[("ps", si)])
                        pv = ptc[0] % 3
                        ptc[0] += 1
                        P.add("act", lambda e, pv=pv, si=si, c0=c0: e.activation(out=PT[pv][:, c0:512], in_=PS[si][:, c0:512], func=AF.Exp),
                              reads=[("ps", si)], writes=[("PT", pv)])
                        if i_ >= 0:
                            P.add("dve", lambda e, pv=pv, c0=c0: e.memset(PT[pv][64:128, c0:c0 + 64], 0.0), writes=[("PT", pv)])

                        def pvfn(e, pv=pv, kb=kb, m0=m0, jq=jq, hv=hv):
                            ins = None
                            for m in range(m0, 4):
                                ins = e.matmul(PS[m][:, 0:129], PT[pv][:, m * 128:(m + 1) * 128], Vh[hv][:, kb, 0:129],
                                               start=(kb == 0), stop=(kb == 4 * jq + m))
                            return ins
                        P.add("pe", pvfn, reads=[("PT", pv), ("Vh", hv)], writes=[("ps", m) for m in range(m0, 4)])
                    ov = obc[0] % 2
                    obc[0] += 1
                    for m in range(4):
                        P.add("dve", lambda e, m=m: e.reciprocal(out=rs[:, m:m + 1], in_=PS[m][:, 128:129]), reads=[("ps", m)], writes=[("rs", m)])
                        P.add("dve", lambda e, m=m, ov=ov: e.tensor_scalar(out=Ob[ov][:, m, :], in0=PS[m][:, 0:128], scalar1=rs[:, m:m + 1], scalar2=None, op0=ALU.mult),
                              reads=[("ps", m), ("rs", m)], writes=[("Ob", ov)])
                    dma("sp", ao_d[jq * 512:(jq + 1) * 512, h * 128:(h + 1) * 128].rearrange("(m p) d -> p m d", p=128), Ob[ov],
                        [("Ob", ov)], [("ao", jq)], ("ob", ov))
            P.barrier()
            A.off = m_persist
            nb = alloc_nstage()
            ob = A.alloc([4, D], F32)
            Ot = A.alloc([4, D], BF16)
            szt = [A.alloc([512], F32) for _ in range(2)]
            yT = A.alloc([16, 512], BF16)
            junk = nb.hb.bitcast(F32)[:, 0:512]
            szc = [0]
            for t in range(NT):
                stage_n(nb, src, t)
                dma("sp", Ot, ao_d[t * 512:(t + 1) * 512, :].rearrange("(b p) f -> p b f", p=128), [("ao", t)], ["Ot"], "ot")
                for n in range(4):
                    W, wk = wload(mla_w_in[j, :, 1024 + n * 512:1024 + (n + 1) * 512])
                    for b in range(4):
                        ps, pk = ps_next()
                        mm_group(ps[:, :], pk, [(nb.hT[:, k, b * 128:(b + 1) * 128], W[:, k, :]) for k in range(16)], ["hT", wk])
                        zv = szc[0] % 2
                        szc[0] += 1
                        P.add("act", lambda e, o=szt[zv], ps=ps: e.activation(out=o, in_=ps[:, :], func=AF.Silu), reads=[pk], writes=[("szt", zv)])
                        osl = Ot[:, b, n * 512:(n + 1) * 512]
                        P.add("dve", lambda e, o=osl, a=szt[zv]: e.tensor_tensor(out=o, in0=o, in1=a, op=ALU.mult),
                              reads=["Ot", ("szt", zv)], writes=["Ot"])
                for b in range(4):
                    for kq in range(4):
                        ps, pk = ps_next()
                        psb = ps[:, :].bitcast(BF16)
                        transposes([psb[:, q * 128:(q + 1) * 128] for q in range(4)], pk,
                                   [Ot[:, b, (kq * 4 + q) * 128:(kq * 4 + q + 1) * 128] for q in range(4)], identb, ["Ot", "identb"])
                        evac_copy(yT[:, kq * 4:(kq + 1) * 4, b * 128:(b + 1) * 128], psb[:, 0:512].rearrange("p (a b) -> p a b", a=4), [pk], ["yT"], eng="dve")
                out_stage(nb, yT, 16, mla_w_out[j], src, dst, t, ob, "ob", junk)
            P.barrier()

        for i in range(depth):
            src = x_d if i == 0 else y_d
            ada_mod(i)
            if i % 2 == 0:
                sgu_layer(i // 2, src, y_d)
                if debug and i == 0:
                    dma("sp", y0_d.rearrange("(p a) d -> p (a d)", p=512), y_d.rearrange("(p a) d -> p (a d)", p=512), [], [], "dbg")
                    P.barrier()
            else:
                mla_layer(i // 2, src, y_d)
        P.barrier()
        P.finalize()

        lanes = list(P.lane_tick.keys())
        esem = {e: ctx.enter_context(nc.semaphore(f"s_{e}")) for e in ENGS}
        lsem = {ln: ctx.enter_context(nc.semaphore(f"l_{i}")) for i, ln in enumerate(lanes)}
        with nc.allow_non_contiguous_dma(reason="small strided parameter loads"):
            with nc.Block() as block:
                @block.sync
                def _(e):
                    P.emit("sp", e, esem, lsem)

                @block.gpsimd
                def _(e):
                    P.emit("pool", e, esem, lsem)

                @block.tensor
                def _(e):
                    P.emit("pe", e, esem, lsem)

                @block.scalar
                def _(e):
                    P.emit("act", e, esem, lsem)

                @block.vector
                def _(e):
                    P.emit("dve", e, esem, lsem)
    return nc


def host_consts(S):
    ident = np.eye(128, dtype=np.float32)
    tch = np.arange(128) // 64
    mask = (tch[None, :] <= tch[:, None]).astype(np.float32)
    pos = np.arange(S, dtype=np.float32)
    inv_freq = (np.float32(10000.0) ** (-np.arange(0, 64, 2, dtype=np.float32) / np.float32(64))).astype(np.float32)
    ang = (pos[:, None] * inv_freq[None, :]).astype(np.float32)
    cos = np.cos(ang).astype(np.float32)
    sin = np.sin(ang).astype(np.float32)
    sc = np.float32(SM_SCALE)
    ttab = np.concatenate([cos.T, cos.T, -sin.T, sin.T], axis=0).astype(np.float32) * sc
    nb = S // 128
    cos_l = np.ascontiguousarray(cos.reshape(nb, 128, 32).transpose(1, 0, 2).reshape(128, nb * 32))
    sin_l = np.ascontiguousarray(sin.reshape(nb, 128, 32).transpose(1, 0, 2).reshape(128, nb * 32))
    return ident, mask, cos_l, sin_l, np.ascontiguousarray(ttab.astype(np.float32))


_CACHE = {}


def make_in_maps(inputs, S, cores):
    ident, mask, cos, sin, ttab = host_consts(S)
    f = lambda a: np.ascontiguousarray(np.asarray(a, dtype=np.float32))
    shared = {
        "ada_w": f(inputs["ada_w"]), "ada_b": f(inputs["ada_b"]), "pre_g": f(inputs["pre_g"]), "post_g": f(inputs["post_g"]),
        "sgu_w_in": f(inputs["sgu_w_in"]),
        "sgu_gT": np.ascontiguousarray(f(inputs["sgu_norm_g"]).reshape(2, 32, 128).transpose(0, 2, 1)),
        "sgu_w_s": f(inputs["sgu_w_s"]), "sgu_b_s": f(inputs["sgu_b_s"]).reshape(2, 2048),
        "sgu_w_out": f(inputs["sgu_w_out"]), "mla_w_in": f(inputs["mla_w_in"]),
        "mla_qg": f(inputs["mla_q_norm_g"]), "mla_kvg": f(inputs["mla_kv_norm_g"]),
        "mla_w_uq": f(inputs["mla_w_uq"]), "mla_w_ukv": f(inputs["mla_w_ukv"]), "mla_w_out": f(inputs["mla_w_out"]),
        "ident": ident, "mask": mask, "cos_tm": cos, "sin_tm": sin, "ttab": ttab,
    }
    x = np.asarray(inputs["x"], dtype=np.float32)
    c = np.asarray(inputs["c"], dtype=np.float32)
    maps = []
    for b in cores:
        m = dict(shared)
        m["x"] = np.ascontiguousarray(x[b, :S])
        m["cT"] = np.ascontiguousarray(c[b].reshape(16, 128).T)
        maps.append(m)
    return maps


def kernel(**inputs):
    S = 4096
    key = (S, 4)
    if key not in _CACHE:
        _CACHE[key] = build(S, 4)
    nc = _CACHE[key]
    maps = make_in_maps(inputs, S, list(range(8)))
    res = run_bass_kernel_spmd(nc, maps, core_ids=list(range(8)))
    out = np.stack([np.asarray(r["y"], dtype=np.float32) for r in res.results], axis=0)
    return out
```

```python
import math
import numpy as np
from contextlib import ExitStack
import concourse.bass as bass
import concourse.mybir as mybir
from concourse.bass_utils import run_bass_kernel_spmd

F32 = mybir.dt.float32
BF16 = mybir.dt.bfloat16
AF = mybir.ActivationFunctionType
ALU = mybir.AluOpType
AX = mybir.AxisListType

D = 2048
E = 4096
H = 16
QR = 448
KVR = 512
EPS = 1e-6
SM_SCALE = 192.0 ** -0.5
ENGS = ("pe", "act", "dve", "pool", "sp")
NW = 3


class Op:
    __slots__ = ("eng", "fn", "waits", "signal", "tick", "lane", "idx", "is_dma")


class Prog:
    def __init__(self):
        self.ops = {e: [] for e in ENGS}
        self.lw = {}
        self.rd = {}
        self.lane_tick = {}
        self.const = set()

    def add(self, eng, fn, reads=(), writes=(), lane=None):
        op = Op()
        op.eng = eng
        op.fn = fn
        op.signal = False
        op.tick = None
        op.lane = lane
        op.is_dma = lane is not None
        op.idx = len(self.ops[eng])
        op.waits = {}
        deps = []
        for k in reads:
            w = self.lw.get(k)
            if w is not None:
                deps.append(w)
        for k in writes:
            w = self.lw.get(k)
            if w is not None:
                deps.append(w)
            deps.extend(self.rd.get(k, ()))
        for a in deps:
            if a is op:
                continue
            if a.is_dma:
                key = ("l", a.lane)
                op.waits[key] = max(op.waits.get(key, 0), self.lane_tick[a.lane])
            else:
                if a.eng == eng and not op.is_dma:
                    if eng == "pe":
                        continue
                    if op.idx - a.idx > 2:
                        continue
                a.signal = True
                key = ("e", a.eng)
                prev = op.waits.get(key)
                if prev is None or prev.idx < a.idx:
                    op.waits[key] = a
        if op.is_dma:
            self.lane_tick[lane] = self.lane_tick.get(lane, 0) + 16
            op.tick = self.lane_tick[lane]
        for k in reads:
            if k not in self.const:
                self.rd.setdefault(k, []).append(op)
        for k in writes:
            self.lw[k] = op
            self.rd[k] = []
        self.ops[eng].append(op)
        return op

    def barrier(self):
        lasts = {}
        for e in ENGS:
            for a in reversed(self.ops[e]):
                if not a.is_dma and a.fn is not None:
                    lasts[e] = a
                    break
        lanes = dict(self.lane_tick)
        for e in ENGS:
            op = Op()
            op.eng = e
            op.fn = None
            op.signal = False
            op.tick = None
            op.lane = None
            op.is_dma = False
            op.idx = len(self.ops[e])
            op.waits = {}
            for x, a in lasts.items():
                a.signal = True
                op.waits[("e", x)] = a
            for ln, v in lanes.items():
                op.waits[("l", ln)] = v
            self.ops[e].append(op)
        self.lw = {}
        self.rd = {}

    def finalize(self):
        for e in ENGS:
            t = 0
            for op in self.ops[e]:
                if (not op.is_dma) and op.signal:
                    t += 1
                    op.tick = t

    def emit(self, engname, eng, esem, lsem):
        waited = {}
        for op in self.ops[engname]:
            for key, v in op.waits.items():
                if key[0] == "e":
                    sem = esem[key[1]]
                    val = v.tick
                else:
                    sem = lsem[key[1]]
                    val = v
                if waited.get(key, 0) >= val:
                    continue
                waited[key] = val
                eng.wait_ge(sem, val)
            if op.fn is None:
                continue
            ins = op.fn(eng)
            if op.is_dma:
                ins.then_inc(lsem[op.lane], 16)
            elif op.signal:
                ins.then_inc(esem[engname], 1)


class Arena:
    def __init__(self, t, size):
        self.t = t
        self.size = size
        self.off = 0

    def alloc(self, free_shape, dtype):
        n = 1
        for s in free_shape:
            n *= s
        ne = n * (2 if dtype == F32 else 1)
        self.off = (self.off + 31) // 32 * 32
        a = self.t[:, self.off:self.off + ne]
        self.off += ne
        assert self.off <= self.size, f"arena overflow {self.off} > {self.size}"
        if dtype == F32:
            a = a.bitcast(F32)
        if len(free_shape) == 2:
            a = a.rearrange("p (a b) -> p a b", a=free_shape[0])
        elif len(free_shape) == 3:
            a = a.rearrange("p (a b c) -> p a b c", a=free_shape[0], b=free_shape[1])
        return a


def build(S, depth, debug=False):
    NT = S // 512
    NB = S // 128
    nc = bass.Bass("TRN2", target_bir_lowering=False)

    def din(name, shape):
        return nc.dram_tensor(name, list(shape), F32, kind="ExternalInput").ap()

    x_d = din("x", [S, D])
    cT_d = din("cT", [128, 16])
    ada_w = din("ada_w", [4, D, 3 * D])
    ada_b = din("ada_b", [4, 3 * D])
    pre_g = din("pre_g", [4, D])
    post_g = din("post_g", [4, D])
    sgu_w_in = din("sgu_w_in", [2, D, 3 * E])
    sgu_gT = din("sgu_gT", [2, 128, 32])
    sgu_w_s = din("sgu_w_s", [2, 16, 128, 128])
    sgu_b_s = din("sgu_b_s", [2, 16 * 128])
    sgu_w_out = din("sgu_w_out", [2, E, D])
    mla_w_in = din("mla_w_in", [2, D, 3072])
    mla_qg = din("mla_qg", [2, QR])
    mla_kvg = din("mla_kvg", [2, KVR])
    mla_w_uq = din("mla_w_uq", [2, QR, 3072])
    mla_w_ukv = din("mla_w_ukv", [2, KVR, 4096])
    mla_w_out = din("mla_w_out", [2, D, D])
    ident_d = din("ident", [128, 128])
    mask_d = din("mask", [128, 128])
    cos_d = din("cos_tm", [128, (S // 128) * 32])
    sin_d = din("sin_tm", [128, (S // 128) * 32])
    ttab_d = din("ttab", [128, S])
    y_d = nc.dram_tensor("y", [S, D], F32, kind="ExternalOutput").ap()
    ao_d = nc.dram_tensor("attn_o", [S, D], BF16, kind="ExternalOutput" if debug else "Internal").ap()
    if debug:
        y0_d = nc.dram_tensor("y0", [S, D], F32, kind="ExternalOutput").ap()
        lat_d = nc.dram_tensor("lat", [128, 9 * S], BF16, kind="ExternalOutput").ap()

    P = Prog()
    ARENA = 106400
    with ExitStack() as ctx:
        arena_t = ctx.enter_context(nc.sbuf_tensor("arena", [128, ARENA], BF16))
        A = Arena(arena_t, ARENA)
        PS = [ctx.enter_context(nc.psum_tensor(f"ps{i}", [128, 512], F32)) for i in range(8)]
        psn = [0]

        def ps_next():
            i = psn[0] % 8
            psn[0] += 1
            return PS[i], ("ps", i)

        identf = A.alloc([128], F32)
        identb = A.alloc([128], BF16)
        maskt = A.alloc([128], F32)
        condT = A.alloc([16], F32)
        cond_bc = A.alloc([16, 128], BF16)
        eps_t = A.alloc([1], F32)
        A_bc = A.alloc([D], F32)
        shift_bc = A.alloc([D], F32)
        G_bc = A.alloc([D], F32)
        small_ap = A.alloc([1024], F32)

        class Small:
            off = 0

            @staticmethod
            def alloc(shape):
                n = 1
                for d_ in shape:
                    n *= d_
                v = small_ap[:, Small.off:Small.off + n]
                Small.off += n
                assert Small.off <= 1024
                if len(shape) == 2:
                    v = v.rearrange("p (a b) -> p a b", a=shape[0])
                elif len(shape) == 3:
                    v = v.rearrange("p (a b c) -> p a b c", a=shape[0], b=shape[1])
                return v
        Wring = [A.alloc([16, 512], BF16) for _ in range(NW)]
        wcnt = [0]
        m_persist = A.off
        for k in ("identf", "identb", "maskt", "cond_bc", "eps"):
            P.const.add(k)

        def wslot():
            s = wcnt[0] % NW
            wcnt[0] += 1
            return s

        def wload(src, kp=128):
            s = wslot()
            K, N = src.shape
            kc = K // kp
            view = Wring[s][0:kp, 0:kc, 0:N]
            srcv = src.rearrange("(k p) n -> p k n", p=kp)
            P.add("pool", lambda e, o=view, i=srcv: e.dma_start(out=o, in_=i),
                  writes=[("W", s)], lane=("W", s))
            return view, ("W", s)

        def dma(eng, out, in_, reads, writes, lane):
            P.add(eng, lambda e, o=out, i=in_: e.dma_start(out=o, in_=i), reads=reads, writes=writes, lane=lane)

        def mm_group(ps_ap, pskey, pairs, reads):
            def fn(e, ps_ap=ps_ap, pairs=pairs):
                n = len(pairs)
                ins = None
                for i, (l, r) in enumerate(pairs):
                    ins = e.matmul(ps_ap, l, r, start=(i == 0), stop=(i == n - 1))
                return ins
            P.add("pe", fn, reads=reads, writes=[pskey])

        def transposes(ps_ap_list, pskey, srcs, ident, reads):
            def fn(e, outs=ps_ap_list, srcs=srcs, ident=ident):
                ins = None
                for o, s_ in zip(outs, srcs):
                    ins = e.transpose(o, s_, ident)
                return ins
            P.add("pe", fn, reads=reads, writes=[pskey])

        evac_rr = [0]

        def evac_copy(out, in_, reads, writes, eng=None):
            evac_rr[0] += 1
            if eng != "dve" and evac_rr[0] % 2 == 0:
                P.add("act", lambda e, o=out, i=in_: e.activation(out=o, in_=i, func=AF.Copy), reads=reads, writes=writes)
            else:
                P.add("dve", lambda e, o=out, i=in_: e.tensor_copy(out=o, in_=i), reads=reads, writes=writes)

        dma("sp", identf, ident_d, [], ["identf"], "c")
        dma("sp", maskt, mask_d, [], ["maskt"], "c")
        dma("sp", condT, cT_d, [], ["condT"], "c")
        P.add("dve", lambda e: e.tensor_copy(out=identb, in_=identf), reads=["identf"], writes=["identb"])
        P.add("dve", lambda e: e.memset(eps_t, EPS), writes=["eps"])
        sig_t = A.alloc([16], F32)
        P.add("act", lambda e: e.activation(out=sig_t, in_=condT, func=AF.Silu), reads=["condT"], writes=["sig"])
        P.add("dve", lambda e: e.tensor_copy(out=cond_bc, in_=sig_t.unsqueeze(2).to_broadcast([128, 16, 128])),
              reads=["sig"], writes=["cond_bc"])
        m_persist = A.off

        def ada_mod(i):
            A.off = m_persist
            m0 = A.off
            btmp = [A.alloc([512], F32) for _ in range(2)]
            gtmp = [A.alloc([512], F32) for _ in range(2)]
            tmp = [A.alloc([512], F32) for _ in range(2)]
            for n in range(12):
                q = n % 2
                W, wk = wload(ada_w[i, :, n * 512:(n + 1) * 512])
                dma("sp", btmp[q], ada_b[i:i + 1, n * 512:(n + 1) * 512].to_broadcast([128, 512]), [], [("btmp", q)], ("sm", q))
                if 4 <= n < 8:
                    dma("sp", gtmp[q], pre_g[i:i + 1, (n - 4) * 512:(n - 3) * 512].to_broadcast([128, 512]), [], [("gtmp", q)], ("sm", q))
                elif n >= 8:
                    dma("sp", gtmp[q], post_g[i:i + 1, (n - 8) * 512:(n - 7) * 512].to_broadcast([128, 512]), [], [("gtmp", q)], ("sm", q))
                ps, pk = ps_next()
                mm_group(ps[:, :], pk, [(cond_bc[:, k, :], W[:, k, :]) for k in range(16)], ["cond_bc", wk])
                if n < 4:
                    o = shift_bc[:, n * 512:(n + 1) * 512]
                    P.add("dve", lambda e, o=o, ps=ps, b=btmp[q]: e.tensor_tensor(out=o, in0=ps[:, :], in1=b, op=ALU.add),
                          reads=[pk, ("btmp", q)], writes=["shift"])
                elif n < 8:
                    o = A_bc[:, (n - 4) * 512:(n - 3) * 512]
                    P.add("dve", lambda e, t=tmp[q], ps=ps, b=btmp[q]: e.tensor_tensor(out=t, in0=ps[:, :], in1=b, op=ALU.add),
                          reads=[pk, ("btmp", q)], writes=[("adt", q)])
                    P.add("dve", lambda e, o=o, t=tmp[q], g=gtmp[q]: e.scalar_tensor_tensor(out=o, in0=t, scalar=1.0, in1=g, op0=ALU.add, op1=ALU.mult),
                          reads=[("adt", q), ("gtmp", q)], writes=["A"])
                else:
                    o = G_bc[:, (n - 8) * 512:(n - 7) * 512]
                    P.add("dve", lambda e, t=tmp[q], ps=ps, b=btmp[q]: e.tensor_tensor(out=t, in0=ps[:, :], in1=b, op=ALU.add),
                          reads=[pk, ("btmp", q)], writes=[("adt", q)])
                    P.add("dve", lambda e, o=o, t=tmp[q], g=gtmp[q]: e.tensor_tensor(out=o, in0=t, in1=g, op=ALU.mult),
                          reads=[("adt", q), ("gtmp", q)], writes=["G"])
            P.barrier()
            A.off = m0

        class NBuf:
            pass

        def alloc_nstage():
            nb = NBuf()
            nb.xblk = [A.alloc([D], F32) for _ in range(2)]
            nb.hb = A.alloc([D], BF16)
            nb.hT = A.alloc([16, 512], BF16)
            Small.off = 0
            nb.ss = Small.alloc([8])
            nb.std = Small.alloc([8])
            nb.rstd = Small.alloc([8])
            nb.sso = Small.alloc([4, 4])
            nb.sst = Small.alloc([4])
            nb.stdo = Small.alloc([4])
            nb.rstdo = Small.alloc([4])
            nb.xc = 0
            return nb

        def rstd_from_ss(ss_ap, std_ap, rstd_ap, inv_n, kss, kstd, krstd):
            P.add("act", lambda e: e.activation(out=std_ap, in_=ss_ap, func=AF.Sqrt, bias=eps_t[:, 0:1], scale=inv_n),
                  reads=[kss, "eps"], writes=[kstd])
            P.add("dve", lambda e: e.reciprocal(out=rstd_ap, in_=std_ap), reads=[kstd], writes=[krstd])

        def stage_n(nb, src, t):
            for b in range(4):
                r0 = (t * 4 + b) * 128
                xs = nb.xc % 2
                nb.xc += 1
                xb = nb.xblk[xs]
                dma("sp", xb, src[r0:r0 + 128, :], [("dram", t * 4 + b)], [("xblk", xs)], ("x", xs))
                ssb = nb.ss[:, b:b + 1]
                P.add("dve", lambda e, o=ssb: e.memset(o, 0.0), writes=[("ss", b)])
                P.add("act", lambda e, xb=xb, o=ssb: e.activation(out=nb.hb, in_=xb, func=AF.Square, accum_out=o),
                      reads=[("xblk", xs), ("ss", b)], writes=["hb", ("ss", b)])
                rstd_from_ss(ssb, nb.std[:, b:b + 1], nb.rstd[:, b:b + 1], 1.0 / D, ("ss", b), ("std", b), ("rstd", b))
                P.add("dve", lambda e, xb=xb, r=nb.rstd[:, b:b + 1]: e.scalar_tensor_tensor(out=xb, in0=xb, scalar=r, in1=A_bc, op0=ALU.mult, op1=ALU.mult),
                      reads=[("xblk", xs), ("rstd", b), "A"], writes=[("xblk", xs)])
                P.add("dve", lambda e, xb=xb: e.tensor_tensor(out=nb.hb, in0=xb, in1=shift_bc, op=ALU.add),
                      reads=[("xblk", xs), "shift"], writes=["hb"])
                for kq in range(4):
                    ps, pk = ps_next()
                    psb = ps[:, :].bitcast(BF16)
                    transposes([psb[:, q * 128:(q + 1) * 128] for q in range(4)], pk,
                               [nb.hb[:, (kq * 4 + q) * 128:(kq * 4 + q + 1) * 128] for q in range(4)], identb, ["hb", "identb"])
                    evac_copy(nb.hT[:, kq * 4:(kq + 1) * 4, b * 128:(b + 1) * 128],
                              psb[:, 0:512].rearrange("p (a b) -> p a b", a=4), [pk], ["hT"])

        def out_stage(nb, yT, nk, wsrc, src, dst, t, ob, obkey, junk):
            sso, sst, stdo, rstdo = nb.sso, nb.sst, nb.stdo, nb.rstdo
            P.add("dve", lambda e: e.memset(sso, 0.0), writes=["sso"])
            for n in range(4):
                Ws = []
                for kh in range(nk // 16):
                    Ws.append(wload(wsrc[kh * 2048:(kh + 1) * 2048, n * 512:(n + 1) * 512]))
                for b in range(4):
                    ps, pk = ps_next()
                    pairs = [(yT[:, k, b * 128:(b + 1) * 128], Ws[k // 16][0][:, k % 16, :]) for k in range(nk)]
                    mm_group(ps[:, :], pk, pairs, ["yT"] + [w[1] for w in Ws])
                    P.add("act", lambda e, o=ob[:, b, n * 512:(n + 1) * 512], ps=ps: e.activation(out=o, in_=ps[:, :], func=AF.Copy),
                          reads=[pk], writes=[(obkey, b)])
                    P.add("act", lambda e, ps=ps, o=sso[:, b, n:n + 1]: e.activation(out=junk, in_=ps[:, :], func=AF.Square, accum_out=o),
                          reads=[pk, "sso"], writes=["hb", "sso"])
            for b in range(4):
                r0 = (t * 4 + b) * 128
                P.add("dve", lambda e, o=sst[:, b:b + 1], i=sso[:, b, :]: e.reduce_sum(out=o, in_=i, axis=AX.X),
                      reads=["sso"], writes=[("sst", b)])
                rstd_from_ss(sst[:, b:b + 1], stdo[:, b:b + 1], rstdo[:, b:b + 1], 1.0 / D, ("sst", b), ("stdo", b), ("rstdo", b))
                xs = nb.xc % 2
                nb.xc += 1
                xb = nb.xblk[xs]
                dma("sp", xb, src[r0:r0 + 128, :], [("dram", t * 4 + b)], [("xblk", xs)], ("x", xs))
                P.add("dve", lambda e, o=ob[:, b, :], r=rstdo[:, b:b + 1]: e.scalar_tensor_tensor(out=o, in0=o, scalar=r, in1=G_bc, op0=ALU.mult, op1=ALU.mult),
                      reads=[(obkey, b), ("rstdo", b), "G"], writes=[(obkey, b)])
                P.add("dve", lambda e, o=ob[:, b, :], xb=xb: e.tensor_tensor(out=o, in0=o, in1=xb, op=ALU.add),
                      reads=[(obkey, b), ("xblk", xs)], writes=[(obkey, b)])
                dma("sp", dst[r0:r0 + 128, :], ob[:, b, :], [(obkey, b)], [("dram", t * 4 + b)], ("out", b % 2))

        def sgu_layer(j, src, dst):
            A.off = m_persist
            nb = alloc_nstage()
            gv = A.alloc([4, E], BF16)
            ob = gv.rearrange("p a b -> p (a b)").bitcast(F32).rearrange("p (a b) -> p a b", a=4)
            wTb = A.alloc([16, 128], BF16)
            bs_bc = A.alloc([16, 128], F32)
            gammaT = Small.alloc([32])
            gu = [A.alloc([512], F32) for _ in range(2)]
            sz = [A.alloc([512], F32) for _ in range(2)]
            t1 = [A.alloc([512], F32) for _ in range(2)]
            stats = Small.alloc([4, 8, 6])
            mv = Small.alloc([4, 2])
            stdv = Small.alloc([4])
            rstdv = Small.alloc([4])
            junk = nb.hb.bitcast(F32)[:, 0:512]
            yT = A.alloc([32, 512], BF16)
            wsl = yT.rearrange("p a b -> p (a b)")[:, 0:4096].bitcast(F32).rearrange("p (a b) -> p a b", a=16)
            dma("sp", wsl, sgu_w_s[j].rearrange("g t s -> t g s"), [], ["yT"], "c")
            dma("sp", bs_bc.rearrange("p a b -> p (a b)"), sgu_b_s[j:j + 1, :].to_broadcast([128, 2048]), [], ["bs"], "c")
            dma("sp", gammaT, sgu_gT[j], [], ["gammaT"], "c")
            P.add("dve", lambda e: e.tensor_tensor(out=wsl, in0=wsl, in1=maskt.unsqueeze(1).to_broadcast([128, 16, 128]), op=ALU.mult),
                  reads=["yT", "maskt"], writes=["yT"])
            for gq in range(4):
                ps, pk = ps_next()
                transposes([ps[:, q * 128:(q + 1) * 128] for q in range(4)], pk,
                           [wsl[:, gq * 4 + q, :] for q in range(4)], identf, ["yT", "identf"])
                evac_copy(wTb[:, gq * 4:(gq + 1) * 4, :], ps[:, :].rearrange("p (a b) -> p a b", a=4), [pk], ["wTb"])
            for t in range(NT):
                stage_n(nb, src, t)
                for n in range(8):
                    W, wk = wload(sgu_w_in[j, :, E + n * 512:E + (n + 1) * 512])
                    for b in range(4):
                        ps, pk = ps_next()
                        mm_group(ps[:, :], pk, [(nb.hT[:, k, b * 128:(b + 1) * 128], W[:, k, :]) for k in range(16)], ["hT", wk])
                        gvs = gv[:, b, n * 512:(n + 1) * 512]
                        P.add("act", lambda e, o=gvs, ps=ps: e.activation(out=o, in_=ps[:, :], func=AF.Gelu),
                              reads=[pk], writes=[("gv", b)])
                        P.add("dve", lambda e, o=stats[:, b, n, :], i=gvs: e.bn_stats(out=o, in_=i),
                              reads=[("gv", b)], writes=[("stats", b)])
                for b in range(4):
                    P.add("dve", lambda e, o=mv[:, b, :], i=stats[:, b, :, :]: e.bn_aggr(out=o, in_=i.rearrange("p a b -> p (a b)")),
                          reads=[("stats", b)], writes=[("mv", b)])
                    rstd_from_ss(mv[:, b, 1:2], stdv[:, b:b + 1], rstdv[:, b:b + 1], 1.0, ("mv", b), ("stdv", b), ("rstdv", b))
                    P.add("dve", lambda e, o=gv[:, b, :], m=mv[:, b, 0:1], r=rstdv[:, b:b + 1]: e.tensor_scalar(out=o, in0=o, scalar1=m, scalar2=r, op0=ALU.subtract, op1=ALU.mult),
                          reads=[("gv", b), ("mv", b), ("rstdv", b)], writes=[("gv", b)])
                for cg in range(8):
                    Wu, wuk = wload(sgu_w_in[j, :, cg * 512:(cg + 1) * 512])
                    Wz, wzk = wload(sgu_w_in[j, :, 2 * E + cg * 512:2 * E + (cg + 1) * 512])
                    for q in range(4):
                        ch = cg * 4 + q
                        g = ch // 2
                        bq = ch % 2
                        psU, pkU = ps_next()
                        mm_group(psU[:, :], pkU, [(Wu[:, k, q * 128:(q + 1) * 128], nb.hT[:, k, :]) for k in range(16)], ["hT", wuk])
                        psZ, pkZ = ps_next()
                        mm_group(psZ[:, :], pkZ, [(Wz[:, k, q * 128:(q + 1) * 128], nb.hT[:, k, :]) for k in range(16)], ["hT", wzk])
                        psM, pkM = ps_next()

                        def mixfn(e, psM=psM, ch=ch, g=g):
                            ins = None
                            for b in range(4):
                                ins = e.matmul(psM[:, b * 128:(b + 1) * 128], gv[:, b, ch * 128:(ch + 1) * 128], wTb[:, g, :], start=True, stop=True)
                            return ins
                        P.add("pe", mixfn, reads=[("gv", 0), ("gv", 1), ("gv", 2), ("gv", 3), "wTb"], writes=[pkM])
                        P.add("act", lambda e, o=gu[bq], ps=psU: e.activation(out=o, in_=ps[:, :], func=AF.Gelu), reads=[pkU], writes=[("gu", bq)])
                        P.add("act", lambda e, o=sz[bq], ps=psZ: e.activation(out=o, in_=ps[:, :], func=AF.Silu), reads=[pkZ], writes=[("sz", bq)])
                        t1v = t1[bq].rearrange("p (a b) -> p a b", a=4)
                        P.add("dve", lambda e, o=t1v, psM=psM, ch=ch, g=g: e.scalar_tensor_tensor(
                            out=o, in0=psM[:, :].rearrange("p (a b) -> p a b", a=4), scalar=gammaT[:, ch:ch + 1],
                            in1=bs_bc[:, g, :].unsqueeze(1).to_broadcast([128, 4, 128]), op0=ALU.mult, op1=ALU.add),
                            reads=[pkM, "gammaT", "bs"], writes=[("t1", bq)])
                        P.add("dve", lambda e, o=t1[bq], a=gu[bq]: e.tensor_tensor(out=o, in0=o, in1=a, op=ALU.mult),
                              reads=[("t1", bq), ("gu", bq)], writes=[("t1", bq)])
                        P.add("dve", lambda e, o=yT[:, ch, :], a=t1[bq], b_=sz[bq]: e.tensor_tensor(out=o, in0=a, in1=b_, op=ALU.mult),
                              reads=[("t1", bq), ("sz", bq)], writes=["yT"])
                out_stage(nb, yT, 32, sgu_w_out[j], src, dst, t, ob, "gv", junk)
            P.barrier()

        def mla_layer(j, src, dst):
            A.off = m_persist
            LAT_OFF = (ARENA - (9 * S + 96)) // 32 * 32
            nb = alloc_nstage()
            cos_t = [A.alloc([4, 32], F32) for _ in range(2)]
            sin_t = [A.alloc([4, 32], F32) for _ in range(2)]
            gq_bc = A.alloc([QR], F32)
            gkv_bc = A.alloc([KVR], F32)
            cqn_b = A.alloc([QR], BF16)
            ckvn_b = A.alloc([KVR], BF16)
            kr2b = A.alloc([128], BF16)
            rt = [A.alloc([32], F32) for _ in range(4)]
            junk = nb.hb.bitcast(F32)[:, 0:512]
            ssq = Small.alloc([2])
            stdq = Small.alloc([2])
            rq = Small.alloc([2])
            assert A.off <= LAT_OFF, (A.off, LAT_OFF)
            A.off = LAT_OFF
            cqT = A.alloc([4, S], BF16)
            ckvT = A.alloc([4, S], BF16)
            Kr2T = A.alloc([S], BF16)
            dma("sp", gq_bc, mla_qg[j:j + 1, :].to_broadcast([128, QR]), [], ["gq"], "c")
            dma("sp", gkv_bc, mla_kvg[j:j + 1, :].to_broadcast([128, KVR]), [], ["gkv"], "c")
            for t in range(NT):
                cv = t % 2
                dma("sp", cos_t[cv].rearrange("p a b -> p (a b)"), cos_d[:, t * 128:(t + 1) * 128], [], [("cos", cv)], ("cs", cv))
                dma("sp", sin_t[cv].rearrange("p a b -> p (a b)"), sin_d[:, t * 128:(t + 1) * 128], [], [("sin", cv)], ("cs", cv))
                stage_n(nb, src, t)
                Wa, wak = wload(mla_w_in[j, :, 0:QR])
                Wb, wbk = wload(mla_w_in[j, :, QR:QR + KVR])
                Wc, wck = wload(mla_w_in[j, :, QR + KVR:QR + KVR + 64])
                for b in range(4):
                    blk = t * 4 + b
                    psA, pkA = ps_next()
                    mm_group(psA[:, 0:QR], pkA, [(nb.hT[:, k, b * 128:(b + 1) * 128], Wa[:, k, :]) for k in range(16)], ["hT", wak])
                    psB, pkB = ps_next()
                    mm_group(psB[:, :], pkB, [(nb.hT[:, k, b * 128:(b + 1) * 128], Wb[:, k, :]) for k in range(16)], ["hT", wbk])
                    psC, pkC = ps_next()
                    mm_group(psC[:, 0:64], pkC, [(nb.hT[:, k, b * 128:(b + 1) * 128], Wc[:, k, :]) for k in range(16)], ["hT", wck])
                    for (ps_, pk_, n_, g_, gk_, dstb, dk, ci) in ((psA, pkA, QR, gq_bc, "gq", cqn_b, "cqn", 0), (psB, pkB, KVR, gkv_bc, "gkv", ckvn_b, "ckvn", 1)):
                        P.add("dve", lambda e, o=ssq[:, ci:ci + 1]: e.memset(o, 0.0), writes=[("ssq", ci)])
                        P.add("act", lambda e, ps_=ps_, n_=n_, o=ssq[:, ci:ci + 1]: e.activation(out=junk[:, 0:n_], in_=ps_[:, 0:n_], func=AF.Square, accum_out=o),
                              reads=[pk_, ("ssq", ci)], writes=["hb", ("ssq", ci)])
                        rstd_from_ss(ssq[:, ci:ci + 1], stdq[:, ci:ci + 1], rq[:, ci:ci + 1], 1.0 / n_, ("ssq", ci), ("stdq", ci), ("rq", ci))
                        P.add("dve", lambda e, ps_=ps_, n_=n_, g_=g_, o=dstb, r=rq[:, ci:ci + 1]: e.scalar_tensor_tensor(
                            out=o, in0=ps_[:, 0:n_], scalar=r, in1=g_, op0=ALU.mult, op1=ALU.mult),
                            reads=[pk_, ("rq", ci), gk_], writes=[dk])
                    ps, pk = ps_next()
                    psb = ps[:, :].bitcast(BF16)
                    transposes([psb[0:112, q * 128:(q + 1) * 128] for q in range(4)], pk,
                               [cqn_b[:, q * 112:(q + 1) * 112] for q in range(4)], identb, ["cqn", "identb"])
                    evac_copy(cqT[0:112, :, blk * 128:(blk + 1) * 128], psb[0:112, 0:512].rearrange("p (a b) -> p a b", a=4), [pk], ["cqT"], eng="dve")
                    ps, pk = ps_next()
                    psb = ps[:, :].bitcast(BF16)
                    transposes([psb[:, q * 128:(q + 1) * 128] for q in range(4)], pk,
                               [ckvn_b[:, q * 128:(q + 1) * 128] for q in range(4)], identb, ["ckvn", "identb"])
                    evac_copy(ckvT[:, :, blk * 128:(blk + 1) * 128], psb[:, 0:512].rearrange("p (a b) -> p a b", a=4), [pk], ["ckvT"], eng="dve")
                    k1 = psC[:, 0:32]
                    k2 = psC[:, 32:64]
                    cs = cos_t[cv][:, b, :]
                    sn = sin_t[cv][:, b, :]
                    P.add("dve", lambda e, k1=k1, cs=cs: e.tensor_tensor(out=rt[0], in0=k1, in1=cs, op=ALU.mult), reads=[pkC, ("cos", cv)], writes=[("rt", 0)])
                    P.add("dve", lambda e, k2=k2, sn=sn: e.tensor_tensor(out=rt[1], in0=k2, in1=sn, op=ALU.mult), reads=[pkC, ("sin", cv)], writes=[("rt", 1)])
                    P.add("dve", lambda e, k2=k2, cs=cs: e.tensor_tensor(out=rt[2], in0=k2, in1=cs, op=ALU.mult), reads=[pkC, ("cos", cv)], writes=[("rt", 2)])
                    P.add("dve", lambda e, k1=k1, sn=sn: e.tensor_tensor(out=rt[3], in0=k1, in1=sn, op=ALU.mult), reads=[pkC, ("sin", cv)], writes=[("rt", 3)])
                    for c0 in (0, 64):
                        P.add("dve", lambda e, c0=c0: e.tensor_tensor(out=kr2b[:, c0:c0 + 32], in0=rt[0], in1=rt[1], op=ALU.subtract),
                              reads=[("rt", 0), ("rt", 1)], writes=["kr2b"])
                        P.add("dve", lambda e, c0=c0: e.tensor_tensor(out=kr2b[:, c0 + 32:c0 + 64], in0=rt[2], in1=rt[3], op=ALU.add),
                              reads=[("rt", 2), ("rt", 3)], writes=["kr2b"])
                    ps, pk = ps_next()
                    psb = ps[:, :].bitcast(BF16)
                    transposes([psb[:, 0:128]], pk, [kr2b[:, :]], identb, ["kr2b", "identb"])
                    evac_copy(Kr2T[:, blk * 128:(blk + 1) * 128], psb[:, 0:128], [pk], ["Kr2T"], eng="dve")
            P.barrier()
            if debug and j == 0:
                dma("sp", lat_d[0:112, 0:4 * S], cqT[0:112].rearrange("p a b -> p (a b)"), [], [], "dbg")
                dma("sp", lat_d[:, 4 * S:8 * S], ckvT.rearrange("p a b -> p (a b)"), [], [], "dbg")
                dma("sp", lat_d[:, 8 * S:9 * S], Kr2T, [], [], "dbg")
                P.barrier()
            A.off = m_persist
            KnT = [A.alloc([S], BF16) for _ in range(2)]
            Vh = [A.alloc([NB, 132], BF16) for _ in range(2)]
            Qn = [A.alloc([512], BF16) for _ in range(2)]
            Qr2 = [A.alloc([512], BF16) for _ in range(2)]
            PT = [A.alloc([512], BF16) for _ in range(3)]
            Ob = [A.alloc([4, 128], BF16) for _ in range(2)]
            Tt = [A.alloc([512], F32) for _ in range(2)]
            assert A.off <= LAT_OFF, (A.off, LAT_OFF)
            Small.off = 0
            rs = Small.alloc([8])
            for v in range(2):
                P.add("dve", lambda e, v=v: e.memset(Vh[v][:, :, 128:129], 1.0), writes=[("Vh", v)])
            sc_c = [0]
            pj_c = [0]
            ptc = [0]
            obc = [0]
            ttc = [0]
            for h in range(H):
                hv = h % 2
                Wk, wkk = wload(mla_w_ukv[j, :, h * 256:(h + 1) * 256])
                for tt in range(NT):
                    pi = 6 + pj_c[0] % 2
                    pj_c[0] += 1
                    mm_group(PS[pi][:, :], ("ps", pi), [(Wk[:, kc, 0:128], ckvT[:, kc, tt * 512:(tt + 1) * 512]) for kc in range(4)], [wkk, "ckvT"])
                    evac_copy(KnT[hv][:, tt * 512:(tt + 1) * 512], PS[pi][:, :], [("ps", pi)], [("KnT", hv)])
                for tt in range(NT):
                    pi = 6 + pj_c[0] % 2
                    pj_c[0] += 1

                    def vfn(e, pi=pi, tt=tt, Wk=Wk):
                        ins = None
                        for bl in range(4):
                            for kc in range(4):
                                ins = e.matmul(PS[pi][:, bl * 128:(bl + 1) * 128], ckvT[:, kc, (tt * 4 + bl) * 128:(tt * 4 + bl + 1) * 128],
                                               Wk[:, kc, 128:256], start=(kc == 0), stop=(kc == 3))
                        return ins
                    P.add("pe", vfn, reads=[wkk, "ckvT"], writes=[("ps", pi)])
                    evac_copy(Vh[hv][:, tt * 4:(tt + 1) * 4, 0:128], PS[pi][:, :].rearrange("p (a b) -> p a b", a=4), [("ps", pi)], [("Vh", hv)])
                s = wslot()
                Wq = Wring[s][0:112, 0:4, 0:256]
                base = h * 192
                srcq = mla_w_uq[j].rearrange("(k p) n -> p k n", p=112)
                P.add("pool", lambda e, o=Wq[:, :, 0:192], i=srcq[:, :, base:base + 192]: e.dma_start(out=o, in_=i), writes=[("W", s)], lane=("W", s))
                P.add("pool", lambda e, o=Wq[:, :, 192:224], i=srcq[:, :, base + 160:base + 192]: e.dma_start(out=o, in_=i), writes=[("W", s)], lane=("W", s))
                P.add("pool", lambda e, o=Wq[:, :, 224:256], i=srcq[:, :, base + 128:base + 160]: e.dma_start(out=o, in_=i), writes=[("W", s)], lane=("W", s))
                wqk = ("W", s)
                for jq in range(NT):
                    qv = jq % 2
                    tv = ttc[0] % 2
                    ttc[0] += 1
                    dma("sp", Tt[tv], ttab_d[:, jq * 512:(jq + 1) * 512], [], [("Tt", tv)], ("Tt", tv))
                    pi = 6 + pj_c[0] % 2
                    pj_c[0] += 1
                    mm_group(PS[pi][:, :], ("ps", pi), [(Wq[:, kc, 0:128], cqT[0:112, kc, jq * 512:(jq + 1) * 512]) for kc in range(4)], [wqk, "cqT"])
                    P.add("act", lambda e, o=Qn[qv], pi=pi: e.activation(out=o, in_=PS[pi][:, :], func=AF.Copy, scale=SM_SCALE),
                          reads=[("ps", pi)], writes=[("Qn", qv)])
                    pi2 = 6 + pj_c[0] % 2
                    pj_c[0] += 1
                    mm_group(PS[pi2][:, :], ("ps", pi2), [(Wq[:, kc, 128:256], cqT[0:112, kc, jq * 512:(jq + 1) * 512]) for kc in range(4)], [wqk, "cqT"])
                    P.add("dve", lambda e, o=Qr2[qv], pi2=pi2, tb=Tt[tv]: e.tensor_tensor(out=o, in0=PS[pi2][:, :], in1=tb, op=ALU.mult),
                          reads=[("ps", pi2), ("Tt", tv)], writes=[("Qr2", qv)])
                    nkb = 4 * (jq + 1)

                    def emit_s(kb, jq=jq, hv=hv, qv=qv):
                        i_ = kb - 4 * jq
                        m0 = max(i_, 0)
                        c0 = m0 * 128
                        si = 4 + sc_c[0] % 2
                        sc_c[0] += 1

                        def sfn(e, si=si, c0=c0, kb=kb, hv=hv, qv=qv):
                            e.matmul(PS[si][:, c0:512], KnT[hv][:, kb * 128:(kb + 1) * 128], Qn[qv][:, c0:512], start=True, stop=False)
                            return e.matmul(PS[si][:, c0:512], Kr2T[:, kb * 128:(kb + 1) * 128], Qr2[qv][:, c0:512], start=False, stop=True)
                        P.add("pe", sfn, reads=[("KnT", hv), "Kr2T", ("Qn", qv), ("Qr2", qv)], writes=[("ps", si)])
                        pv = ptc[0] % 3
                        ptc[0] += 1
                        P.add("act", lambda e, pv=pv, si=si, c0=c0: e.activation(out=PT[pv][:, c0:512], in_=PS[si][:, c0:512], func=AF.Exp),
                              reads=[("ps", si)], writes=[("PT", pv)])
                        if i_ >= 0:
                            P.add("dve", lambda e, pv=pv, c0=c0: e.memset(PT[pv][64:128, c0:c0 + 64], 0.0), writes=[("PT", pv)])
                        return pv, m0

                    pend = emit_s(0)
                    for kb in range(nkb):
                        pv, m0 = pend
                        if kb + 1 < nkb:
                            pend = emit_s(kb + 1)

                        def pvfn(e, pv=pv, kb=kb, m0=m0, jq=jq, hv=hv):
                            ins = None
                            for m in range(m0, 4):
                                ins = e.matmul(PS[m][:, 0:129], PT[pv][:, m * 128:(m + 1) * 128], Vh[hv][:, kb, 0:129],
                                               start=(kb == 0), stop=(kb == 4 * jq + m))
                            return ins
                        P.add("pe", pvfn, reads=[("PT", pv), ("Vh", hv)], writes=[("ps", m) for m in range(m0, 4)])
                    ov = obc[0] % 2
                    obc[0] += 1
                    for m in range(4):
                        P.add("dve", lambda e, m=m: e.reciprocal(out=rs[:, m:m + 1], in_=PS[m][:, 128:129]), reads=[("ps", m)], writes=[("rs", m)])
                        P.add("dve", lambda e, m=m, ov=ov: e.tensor_scalar(out=Ob[ov][:, m, :], in0=PS[m][:, 0:128], scalar1=rs[:, m:m + 1], scalar2=None, op0=ALU.mult),
                              reads=[("ps", m), ("rs", m)], writes=[("Ob", ov)])
                    dma("sp", ao_d[jq * 512:(jq + 1) * 512, h * 128:(h + 1) * 128].rearrange("(m p) d -> p m d", p=128), Ob[ov],
                        [("Ob", ov)], [("ao", jq)], ("ob", ov))
            P.barrier()
            A.off = m_persist
            nb = alloc_nstage()
            ob = A.alloc([4, D], F32)
            Ot = A.alloc([4, D], BF16)
            szt = [A.alloc([512], F32) for _ in range(2)]
            yT = A.alloc([16, 512], BF16)
            junk = nb.hb.bitcast(F32)[:, 0:512]
            szc = [0]
            for t in range(NT):
                stage_n(nb, src, t)
                dma("sp", Ot, ao_d[t * 512:(t + 1) * 512, :].rearrange("(b p) f -> p b f", p=128), [("ao", t)], ["Ot"], "ot")
                for n in range(4):
                    W, wk = wload(mla_w_in[j, :, 1024 + n * 512:1024 + (n + 1) * 512])
                    for b in range(4):
                        ps, pk = ps_next()
                        mm_group(ps[:, :], pk, [(nb.hT[:, k, b * 128:(b + 1) * 128], W[:, k, :]) for k in range(16)], ["hT", wk])
                        zv = szc[0] % 2
                        szc[0] += 1
                        P.add("act", lambda e, o=szt[zv], ps=ps: e.activation(out=o, in_=ps[:, :], func=AF.Silu), reads=[pk], writes=[("szt", zv)])
                        osl = Ot[:, b, n * 512:(n + 1) * 512]
                        P.add("dve", lambda e, o=osl, a=szt[zv]: e.tensor_tensor(out=o, in0=o, in1=a, op=ALU.mult),
                              reads=["Ot", ("szt", zv)], writes=["Ot"])
                for b in range(4):
                    for kq in range(4):
                        ps, pk = ps_next()
                        psb = ps[:, :].bitcast(BF16)
                        transposes([psb[:, q * 128:(q + 1) * 128] for q in range(4)], pk,
                                   [Ot[:, b, (kq * 4 + q) * 128:(kq * 4 + q + 1) * 128] for q in range(4)], identb, ["Ot", "identb"])
                        evac_copy(yT[:, kq * 4:(kq + 1) * 4, b * 128:(b + 1) * 128], psb[:, 0:512].rearrange("p (a b) -> p a b", a=4), [pk], ["yT"], eng="dve")
                out_stage(nb, yT, 16, mla_w_out[j], src, dst, t, ob, "ob", junk)
            P.barrier()

        for i in range(depth):
            src = x_d if i == 0 else y_d
            ada_mod(i)
            if i % 2 == 0:
                sgu_layer(i // 2, src, y_d)
                if debug and i == 0:
                    dma("sp", y0_d.rearrange("(p a) d -> p (a d)", p=512), y_d.rearrange("(p a) d -> p (a d)", p=512), [], [], "dbg")
                    P.barrier()
            else:
                mla_layer(i // 2, src, y_d)
        P.barrier()
        P.finalize()

        lanes = list(P.lane_tick.keys())
        esem = {e: ctx.enter_context(nc.semaphore(f"s_{e}")) for e in ENGS}
        lsem = {ln: ctx.enter_context(nc.semaphore(f"l_{i}")) for i, ln in enumerate(lanes)}
        with nc.allow_non_contiguous_dma(reason="small strided parameter loads"):
            with nc.Block() as block:
                @block.sync
                def _(e):
                    P.emit("sp", e, esem, lsem)

                @block.gpsimd
                def _(e):
                    P.emit("pool", e, esem, lsem)

                @block.tensor
                def _(e):
                    P.emit("pe", e, esem, lsem)

                @block.scalar
                def _(e):
                    P.emit("act", e, esem, lsem)

                @block.vector
                def _(e):
                    P.emit("dve", e, esem, lsem)
    return nc


def host_consts(S):
    ident = np.eye(128, dtype=np.float32)
    tch = np.arange(128) // 64
    mask = (tch[None, :] <= tch[:, None]).astype(np.float32)
    pos = np.arange(S, dtype=np.float32)
    inv_freq = (np.float32(10000.0) ** (-np.arange(0, 64, 2, dtype=np.float32) / np.float32(64))).astype(np.float32)
    ang = (pos[:, None] * inv_freq[None, :]).astype(np.float32)
    cos = np.cos(ang).astype(np.float32)
    sin = np.sin(ang).astype(np.float32)
    sc = np.float32(SM_SCALE)
    ttab = np.concatenate([cos.T, cos.T, -sin.T, sin.T], axis=0).astype(np.float32) * sc
    nb = S // 128
    cos_l = np.ascontiguousarray(cos.reshape(nb, 128, 32).transpose(1, 0, 2).reshape(128, nb * 32))
    sin_l = np.ascontiguousarray(sin.reshape(nb, 128, 32).transpose(1, 0, 2).reshape(128, nb * 32))
    return ident, mask, cos_l, sin_l, np.ascontiguousarray(ttab.astype(np.float32))


_CACHE = {}


def make_in_maps(inputs, S, cores):
    ident, mask, cos, sin, ttab = host_consts(S)
    f = lambda a: np.ascontiguousarray(np.asarray(a, dtype=np.float32))
    shared = {
        "ada_w": f(inputs["ada_w"]), "ada_b": f(inputs["ada_b"]), "pre_g": f(inputs["pre_g"]), "post_g": f(inputs["post_g"]),
        "sgu_w_in": f(inputs["sgu_w_in"]),
        "sgu_gT": np.ascontiguousarray(f(inputs["sgu_norm_g"]).reshape(2, 32, 128).transpose(0, 2, 1)),
        "sgu_w_s": f(inputs["sgu_w_s"]), "sgu_b_s": f(inputs["sgu_b_s"]).reshape(2, 2048),
        "sgu_w_out": f(inputs["sgu_w_out"]), "mla_w_in": f(inputs["mla_w_in"]),
        "mla_qg": f(inputs["mla_q_norm_g"]), "mla_kvg": f(inputs["mla_kv_norm_g"]),
        "mla_w_uq": f(inputs["mla_w_uq"]), "mla_w_ukv": f(inputs["mla_w_ukv"]), "mla_w_out": f(inputs["mla_w_out"]),
        "ident": ident, "mask": mask, "cos_tm": cos, "sin_tm": sin, "ttab": ttab,
    }
    x = np.asarray(inputs["x"], dtype=np.float32)
    c = np.asarray(inputs["c"], dtype=np.float32)
    maps = []
    for b in cores:
        m = dict(shared)
        m["x"] = np.ascontiguousarray(x[b, :S])
        m["cT"] = np.ascontiguousarray(c[b].reshape(16, 128).T)
        maps.append(m)
    return maps


def kernel(**inputs):
    S = 4096
    key = (S, 4)
    if key not in _CACHE:
        _CACHE[key] = build(S, 4)
    nc = _CACHE[key]
    maps = make_in_maps(inputs, S, list(range(8)))
    res = run_bass_kernel_spmd(nc, maps, core_ids=list(range(8)))
    out = np.stack([np.asarray(r["y"], dtype=np.float32) for r in res.results], axis=0)
    return out
```
